# Optimizing a Trainium2 kernel written in Bass

```python
import math
import numpy as np
import jax
import jax.numpy as jnp
from jax import lax

D_MODEL = 1024
BATCH = 16
SEQ = 4096
DEPTH = 4

GRID_W = 64
CTX_LEN = 256
EPS = 1e-6
CONV_W = 3

SSD_HEADS = 8
SSD_HEAD_DIM = 64
SSD_GROUPS = 2
SSD_STATE = 64
SSD_CHUNK = 128
SSD_HPG = SSD_HEADS // SSD_GROUPS
SSD_INNER = SSD_HEADS * SSD_HEAD_DIM
SSD_CONV_DIM = SSD_INNER + 2 * SSD_GROUPS * SSD_STATE

NA_HEADS = 4
NA_HEAD_DIM = 64
NA_INNER = NA_HEADS * NA_HEAD_DIM
NA_ROWS = 8
NA_COLS = 16
NA_QBLOCK = 16
NA_KBLOCK = NA_QBLOCK + NA_COLS

HG_HEADS = 4
HG_KDIM = 64
HG_VDIM = 64
HG_K = HG_HEADS * HG_KDIM
HG_V = HG_HEADS * HG_VDIM
HG_CHUNK = 64

D_MIX = SSD_INNER + NA_INNER + HG_V
D_FF = 2816
IN_WIDTHS = (SSD_INNER, SSD_CONV_DIM, 2 * SSD_HEADS, 3 * NA_INNER, HG_K, 2 * HG_K, HG_V, HG_V)
D_IN = sum(IN_WIDTHS)

kernel_name = 'hybrid_ssd_natten_hgrn2_dit'


def rmsnorm(x, w):
    xf = x.astype(jnp.float32)
    y = xf * lax.rsqrt(jnp.mean(xf * xf, axis=-1, keepdims=True) + EPS)
    return (y * w.astype(jnp.float32)).astype(x.dtype)


def modulate(h, shift, scale):
    return h * (1 + scale) + shift


def dwconv_centred(x, w, b):
    k_w = w.shape[0]
    pad = k_w // 2
    length = x.shape[1]
    xp = jnp.pad(x, ((0, 0), (pad, pad), (0, 0)))
    y = b + xp[:, 0:length] * w[0]
    for k in range(1, k_w):
        y = y + xp[:, k:k + length] * w[k]
    return y


def _flip(t):
    return jnp.flip(t, axis=1)


def _to_chunks(t, chunk):
    bsz, length = t.shape[:2]
    return jnp.moveaxis(t.reshape((bsz, length // chunk, chunk) + t.shape[2:]), 1, 0)


def _from_chunks(t):
    nc, bsz, chunk = t.shape[:3]
    return jnp.moveaxis(t, 0, 1).reshape((bsz, nc * chunk) + t.shape[3:])


def ssd_scan(xdt, bm, cm, loga, h0):
    causal = np.tril(np.ones((SSD_CHUNK, SSD_CHUNK), bool))[None, :, :, None, None]

    def step(h, inp):
        x_c, b_c, c_c, a_c = inp
        cum = jnp.cumsum(a_c, axis=1)
        seg = jnp.where(causal, cum[:, :, None] - cum[:, None], -jnp.inf)
        decay = jnp.exp(seg)
        scores = jnp.einsum('btgn,bsgn->btsg', c_c, b_c)
        y = jnp.einsum('btsg,btsgh,bsghp->btghp', scores, decay, x_c)
        y = y + jnp.einsum('btgn,bghnp->btghp', c_c, h) * jnp.exp(cum)[..., None]
        to_end = jnp.exp(cum[:, -1:] - cum)
        h = jnp.exp(cum[:, -1])[..., None, None] * h + jnp.einsum('bsgn,bsgh,bsghp->bghnp', b_c, to_end, x_c)
        return h, y

    h, ys = lax.scan(step, h0, (_to_chunks(xdt, SSD_CHUNK), _to_chunks(bm, SSD_CHUNK),
                                _to_chunks(cm, SSD_CHUNK), _to_chunks(loga, SSD_CHUNK)))
    return _from_chunks(ys), h


def gla_scan(q, k, v, log_f, s0):
    causal = np.tril(np.ones((HG_CHUNK, HG_CHUNK), bool))[None, :, :, None, None]

    def step(s, inp):
        q_c, k_c, v_c, g_c = inp
        cum = jnp.cumsum(g_c, axis=1)
        seg = jnp.where(causal, cum[:, :, None] - cum[:, None], -jnp.inf)
        scores = jnp.einsum('bthk,bshk,btshk->bhts', q_c, k_c, jnp.exp(seg))
        o = jnp.einsum('bhts,bshv->bthv', scores, v_c)
        o = o + jnp.einsum('bthk,bhkv->bthv', q_c * jnp.exp(cum), s)
        s = jnp.exp(cum[:, -1])[..., None] * s + jnp.einsum('bshk,bshv->bhkv', k_c * jnp.exp(cum[:, -1:] - cum), v_c)
        return s, o

    s, os_ = lax.scan(step, s0, (_to_chunks(q, HG_CHUNK), _to_chunks(k, HG_CHUNK),
                                 _to_chunks(v, HG_CHUNK), _to_chunks(log_f, HG_CHUNK)))
    return _from_chunks(os_), s


def ssd_mixer(parts_ctx, parts_lat, conv_w, conv_b, dt_bias, a_log, d_skip, norm_w, need_ctx):
    a_neg = -jnp.exp(a_log.astype(jnp.float32))

    def prep(parts):
        z, xbc, dt_raw = parts
        bsz, length, _ = z.shape
        xbc = jax.nn.silu(dwconv_centred(xbc, conv_w, conv_b)).astype(jnp.float32)
        xs, bm, cm = jnp.split(xbc, [SSD_INNER, SSD_INNER + SSD_GROUPS * SSD_STATE], axis=-1)
        xs = xs.reshape(bsz, length, SSD_GROUPS, SSD_HPG, SSD_HEAD_DIM)
        bm = bm.reshape(bsz, length, SSD_GROUPS, SSD_STATE)
        cm = cm.reshape(bsz, length, SSD_GROUPS, SSD_STATE)
        dt = jax.nn.softplus(dt_raw.astype(jnp.float32).reshape(bsz, length, 2, SSD_HEADS)
                             + dt_bias.astype(jnp.float32))
        loga = (dt * a_neg).reshape(bsz, length, 2, SSD_GROUPS, SSD_HPG)
        dt = dt.reshape(bsz, length, 2, SSD_GROUPS, SSD_HPG)
        fwd = (xs * dt[:, :, 0, :, :, None], bm, cm, loga[:, :, 0])
        bwd = tuple(_flip(t) for t in (xs * dt[:, :, 1, :, :, None], bm, cm, loga[:, :, 1]))
        return z, xs, fwd, bwd

    z_c, xs_c, fwd_c, bwd_c = prep(parts_ctx)
    z_l, xs_l, fwd_l, bwd_l = prep(parts_lat)
    h0 = jnp.zeros((z_l.shape[0], SSD_GROUPS, SSD_HPG, SSD_STATE, SSD_HEAD_DIM), jnp.float32)
    yf_c, hf = ssd_scan(*fwd_c, h0)
    yb_c, hb = ssd_scan(*bwd_c, h0)
    yf_l, _ = ssd_scan(*fwd_l, hf)
    yb_l, _ = ssd_scan(*bwd_l, hb)
    d_h = d_skip.astype(jnp.float32).reshape(SSD_GROUPS, SSD_HPG, 1)

    def finish(z, xs, yf, yb):
        y = (yf + _flip(yb) + xs * d_h).reshape(z.shape).astype(z.dtype)
        return rmsnorm(y * jax.nn.silu(z), norm_w)

    y_ctx = finish(z_c, xs_c, yf_c, yb_c) if need_ctx else None
    return y_ctx, finish(z_l, xs_l, yf_l, yb_l)


def na_mixer(qkv_ctx, qkv_lat, rpb, need_ctx):
    scale = NA_HEAD_DIM ** -0.5

    def heads(t):
        bsz, length, _ = t.shape
        return t.reshape(bsz, length, NA_HEADS, NA_HEAD_DIM).transpose(0, 2, 1, 3)

    q_c, k_c, v_c = (heads(t) for t in jnp.split(qkv_ctx, 3, axis=-1))
    q_l, k_l, v_l = (heads(t) for t in jnp.split(qkv_lat, 3, axis=-1))
    o_ctx = None
    if need_ctx:
        s = jnp.einsum('bhqd,bhkd->bhqk', q_c, k_c).astype(jnp.float32) * scale
        o = jnp.einsum('bhqk,bhkd->bhqd', jax.nn.softmax(s, axis=-1).astype(v_c.dtype), v_c)
        o_ctx = o.transpose(0, 2, 1, 3).reshape(o.shape[0], o.shape[2], NA_INNER)

    bsz, _, length, _ = q_l.shape
    rows = length // GRID_W
    kr = min(NA_ROWS, rows)
    nb = GRID_W // NA_QBLOCK
    qcol = np.arange(GRID_W).reshape(nb, NA_QBLOCK)
    qcol_start = np.clip(qcol - NA_COLS // 2, 0, GRID_W - NA_COLS)
    kcol = (np.clip(np.arange(nb) * NA_QBLOCK - NA_COLS // 2, 0, GRID_W - NA_KBLOCK)[:, None]
            + np.arange(NA_KBLOCK))
    in_win = ((kcol[:, None, :] >= qcol_start[:, :, None])
              & (kcol[:, None, :] < qcol_start[:, :, None] + NA_COLS))
    col_idx = np.clip(kcol[:, None, :] - qcol[:, :, None] + NA_COLS - 1, 0, 2 * NA_COLS - 2)
    mask = jnp.asarray(in_win[:, :, None, :])
    rpb32 = rpb.astype(jnp.float32)
    k_grid = k_l.reshape(bsz, NA_HEADS, rows, GRID_W, NA_HEAD_DIM)
    v_grid = v_l.reshape(bsz, NA_HEADS, rows, GRID_W, NA_HEAD_DIM)
    q_rows = jnp.moveaxis(q_l.reshape(bsz, NA_HEADS, rows, nb, NA_QBLOCK, NA_HEAD_DIM), 2, 0)
    n_loc = kr * NA_KBLOCK

    def one_row(args):
        i, q = args
        r0 = jnp.clip(i - kr // 2, 0, rows - kr)
        k_blk = lax.dynamic_slice_in_dim(k_grid, r0, kr, axis=2)[:, :, :, kcol]
        v_blk = lax.dynamic_slice_in_dim(v_grid, r0, kr, axis=2)[:, :, :, kcol]
        bias = rpb32[:, r0 + jnp.arange(kr) - i + NA_ROWS - 1][:, :, col_idx]
        bias = bias.transpose(0, 2, 3, 1, 4)
        s_loc = jnp.einsum('bhnqd,bhrnkd->bhnqrk', q, k_blk).astype(jnp.float32) * scale
        s_loc = jnp.where(mask, s_loc + bias, -jnp.inf)
        s_ctx = jnp.einsum('bhnqd,bhcd->bhnqc', q, k_c).astype(jnp.float32) * scale
        s_all = jnp.concatenate([s_loc.reshape(s_loc.shape[:4] + (n_loc,)), s_ctx], axis=-1)
        p = jax.nn.softmax(s_all, axis=-1).astype(q.dtype)
        p_loc = p[..., :n_loc].reshape(s_loc.shape)
        return (jnp.einsum('bhnqrk,bhrnkd->bhnqd', p_loc, v_blk)
                + jnp.einsum('bhnqc,bhcd->bhnqd', p[..., n_loc:], v_c))

    o_rows = lax.map(one_row, (jnp.arange(rows), q_rows))
    o_lat = jnp.moveaxis(o_rows, 0, 2).reshape(bsz, NA_HEADS, length, NA_HEAD_DIM)
    o_lat = o_lat.transpose(0, 2, 1, 3).reshape(bsz, length, NA_INNER)
    return o_ctx, o_lat


def hgrn_mixer(parts_ctx, parts_lat, lb, norm_w, need_ctx):
    lb_h = lb.reshape(HG_HEADS, HG_KDIM)
    log_lb = jnp.log(lb_h)
    log_1mlb = jnp.log1p(-lb_h)

    def prep(parts):
        q, f, i, g = parts
        bsz, length, _ = q.shape
        q = jax.nn.silu(q.astype(jnp.float32)).reshape(bsz, length, HG_HEADS, HG_KDIM)
        f = f.astype(jnp.float32).reshape(bsz, length, 2, HG_HEADS, HG_KDIM)
        log_f = jnp.logaddexp(log_lb, log_1mlb + jax.nn.log_sigmoid(f))
        k = (1.0 - lb_h) * jax.nn.sigmoid(-f)
        v = i.astype(jnp.float32).reshape(bsz, length, HG_HEADS, HG_VDIM)
        fwd = (q, k[:, :, 0], v, log_f[:, :, 0])
        bwd = tuple(_flip(t) for t in (q, k[:, :, 1], v, log_f[:, :, 1]))
        return g, fwd, bwd

    g_c, fwd_c, bwd_c = prep(parts_ctx)
    g_l, fwd_l, bwd_l = prep(parts_lat)
    s0 = jnp.zeros((g_l.shape[0], HG_HEADS, HG_KDIM, HG_VDIM), jnp.float32)
    of_c, sf = gla_scan(*fwd_c, s0)
    ob_c, sb = gla_scan(*bwd_c, s0)
    of_l, _ = gla_scan(*fwd_l, sf)
    ob_l, _ = gla_scan(*bwd_l, sb)

    def finish(g, of, ob):
        o = (of + _flip(ob)).astype(g.dtype)
        o = rmsnorm(o, norm_w).reshape(g.shape)
        return o * jax.nn.silu(g)

    o_ctx = finish(g_c, of_c, ob_c) if need_ctx else None
    return o_ctx, finish(g_l, of_l, ob_l)


def mixer_block(u_ctx, u_lat, w_in, w_out, ssd_conv_w, ssd_conv_b, ssd_dt_bias, ssd_a_log, ssd_d,
                ssd_norm, na_rpb, hg_lb, hg_norm, need_ctx):
    cuts = [int(v) for v in np.cumsum(IN_WIDTHS)[:-1]]
    pc = jnp.split(u_ctx @ w_in, cuts, axis=-1)
    pl = jnp.split(u_lat @ w_in, cuts, axis=-1)
    ssd_c, ssd_l = ssd_mixer(pc[0:3], pl[0:3], ssd_conv_w, ssd_conv_b, ssd_dt_bias, ssd_a_log, ssd_d,
                             ssd_norm, need_ctx)
    na_c, na_l = na_mixer(pc[3], pl[3], na_rpb, need_ctx)
    hg_c, hg_l = hgrn_mixer(pc[4:8], pl[4:8], hg_lb, hg_norm, need_ctx)
    out_lat = jnp.concatenate([ssd_l, na_l, hg_l], axis=-1) @ w_out
    out_ctx = jnp.concatenate([ssd_c, na_c, hg_c], axis=-1) @ w_out if need_ctx else None
    return out_ctx, out_lat


def conv_ffn(h, w_up, conv_w, conv_b, w_down):
    gate, up = jnp.split(h @ w_up, 2, axis=-1)
    gate = dwconv_centred(gate, conv_w, conv_b)
    return (jax.nn.silu(gate) * up) @ w_down


def setup_inputs(seed: int = 0) -> dict:
    key = jax.random.key(seed)
    ks = jax.random.split(key, 26)
    f32 = jnp.float32

    def nrm(k, shape, scale):
        return jax.random.normal(k, shape, f32) * scale

    def gain(k, shape):
        return 1.0 + 0.05 * jax.random.normal(k, shape, f32)

    dt0 = jnp.exp(jax.random.uniform(ks[11], (DEPTH, 2, SSD_HEADS), f32, math.log(1e-3), math.log(1e-1)))
    return {
        'x': nrm(ks[0], (BATCH, SEQ, D_MODEL), 1.0),
        'c': nrm(ks[1], (BATCH, D_MODEL), 1.0),
        'ctx': nrm(ks[2], (BATCH, CTX_LEN, D_MODEL), 1.0),
        'c_ctx': nrm(ks[3], (D_MODEL,), 1.0),
        'ada_w': nrm(ks[4], (DEPTH, D_MODEL, 6 * D_MODEL), 0.5 * D_MODEL ** -0.5),
        'ada_b': nrm(ks[5], (DEPTH, 6 * D_MODEL), 0.02),
        'norm_mix_pre': gain(ks[6], (DEPTH, D_MODEL)),
        'norm_mix_post': gain(ks[7], (DEPTH, D_MODEL)),
        'norm_ffn_pre': gain(ks[8], (DEPTH, D_MODEL)),
        'norm_ffn_post': gain(ks[9], (DEPTH, D_MODEL)),
        'w_in': nrm(ks[10], (DEPTH, D_MODEL, D_IN), D_MODEL ** -0.5),
        'ssd_conv_w': nrm(ks[12], (DEPTH, CONV_W, SSD_CONV_DIM), CONV_W ** -0.5),
        'ssd_conv_b': nrm(ks[13], (DEPTH, SSD_CONV_DIM), 0.02),
        'ssd_dt_bias': dt0 + jnp.log(-jnp.expm1(-dt0)),
        'ssd_a_log': jnp.log(jax.random.uniform(ks[14], (DEPTH, 2, SSD_HEADS), f32, 1.0, 16.0)),
        'ssd_d': 1.0 + 0.1 * jax.random.normal(ks[15], (DEPTH, SSD_HEADS), f32),
        'ssd_norm': gain(ks[16], (DEPTH, SSD_INNER)),
        'na_rpb': nrm(ks[17], (DEPTH, NA_HEADS, 2 * NA_ROWS - 1, 2 * NA_COLS - 1), 0.1),
        'hg_lb_logits': nrm(ks[18], (DEPTH, HG_K), 0.1),
        'hg_norm': gain(ks[19], (DEPTH, HG_VDIM)),
        'w_out': nrm(ks[20], (DEPTH, D_MIX, D_MODEL), D_MIX ** -0.5),
        'ffn_w_up': nrm(ks[21], (DEPTH, D_MODEL, 2 * D_FF), D_MODEL ** -0.5),
        'ffn_conv_w': nrm(ks[22], (DEPTH, CONV_W, D_FF), CONV_W ** -0.5),
        'ffn_conv_b': nrm(ks[23], (DEPTH, D_FF), 0.02),
        'ffn_w_down': nrm(ks[24], (DEPTH, D_FF, D_MODEL), D_FF ** -0.5),
    }


def reference(x, c, ctx, c_ctx, ada_w, ada_b, norm_mix_pre, norm_mix_post, norm_ffn_pre, norm_ffn_post,
              w_in, ssd_conv_w, ssd_conv_b, ssd_dt_bias, ssd_a_log, ssd_d, ssd_norm, na_rpb,
              hg_lb_logits, hg_norm, w_out, ffn_w_up, ffn_conv_w, ffn_conv_b, ffn_w_down):
    lb_all = jnp.cumsum(jax.nn.softmax(hg_lb_logits.astype(jnp.float32), axis=0), axis=0)
    lb_all = lb_all - lb_all[0]
    s_lat = jax.nn.silu(c)[:, None, :]
    s_ctx = jax.nn.silu(c_ctx)[None, None, :]
    h_lat, h_ctx = x, ctx
    for l in range(DEPTH):
        need_ctx = l < DEPTH - 1
        m_lat = jnp.split(s_lat @ ada_w[l] + ada_b[l], 6, axis=-1)
        m_ctx = jnp.split(s_ctx @ ada_w[l] + ada_b[l], 6, axis=-1)
        u_ctx = modulate(rmsnorm(h_ctx, norm_mix_pre[l]), m_ctx[0], m_ctx[1])
        u_lat = modulate(rmsnorm(h_lat, norm_mix_pre[l]), m_lat[0], m_lat[1])
        a_ctx, a_lat = mixer_block(u_ctx, u_lat, w_in[l], w_out[l], ssd_conv_w[l], ssd_conv_b[l],
                                   ssd_dt_bias[l], ssd_a_log[l], ssd_d[l], ssd_norm[l], na_rpb[l],
                                   lb_all[l], hg_norm[l], need_ctx)
        h_lat = h_lat + m_lat[2] * rmsnorm(a_lat, norm_mix_post[l])
        v_lat = modulate(rmsnorm(h_lat, norm_ffn_pre[l]), m_lat[3], m_lat[4])
        f_lat = conv_ffn(v_lat, ffn_w_up[l], ffn_conv_w[l], ffn_conv_b[l], ffn_w_down[l])
        h_lat = h_lat + m_lat[5] * rmsnorm(f_lat, norm_ffn_post[l])
        if need_ctx:
            h_ctx = h_ctx + m_ctx[2] * rmsnorm(a_ctx, norm_mix_post[l])
            v_ctx = modulate(rmsnorm(h_ctx, norm_ffn_pre[l]), m_ctx[3], m_ctx[4])
            f_ctx = conv_ffn(v_ctx, ffn_w_up[l], ffn_conv_w[l], ffn_conv_b[l], ffn_w_down[l])
            h_ctx = h_ctx + m_ctx[5] * rmsnorm(f_ctx, norm_ffn_post[l])
    return h_lat
```

```python
import numpy as np
import concourse.bass as bass
import concourse.mybir as mybir
from concourse.bass_utils import run_bass_kernel_spmd

F32 = mybir.dt.float32
BF16 = mybir.dt.bfloat16
AF = mybir.ActivationFunctionType
ALU = mybir.AluOpType
AX = mybir.AxisListType

D = 1024
T = 4352
UTW = T + 3
DFF = 2816
EPS = 1e-6


def ucol(p):
    return p + 1 if p < 256 else p + 2


class _Rec:
    def __init__(self):
        self.call = None

    def __getattr__(self, name):
        def f(*a, **k):
            self.call = (name, a, k)
            return self
        return f


class Sched:
    ENG = ('pe', 'act', 'dve', 'pool', 'sp')

    def __init__(self, nc, nd_sp=8, nd_pool=6):
        self.nc = nc
        self.thunks = {k: [] for k in self.ENG}
        self.cnt = {k: 0 for k in self.ENG}
        self.NSET = 16
        self.semobj = {}
        self.semval = {}
        for i in range(self.NSET):
            for k in ('pe', 'act', 'dve', 'pool'):
                nm = 's_%s_%d' % (k, i)
                self.semobj[nm] = nc.alloc_semaphore(nm)
                self.semval[nm] = 0
        self.set = 0
        self.own = {k: 's_%s_0' % k for k in ('pe', 'act', 'dve', 'pool')}
        self.sem = {k: self.semobj[self.own[k]] for k in ('pe', 'act', 'dve', 'pool')}
        self.waited = {k: {} for k in self.ENG}
        self.regs = {}
        self.dq = {}
        for q, n in (('sp', nd_sp), ('pool', nd_pool)):
            names = ['d%s%d' % (q, i) for i in range(n)]
            for nm in names:
                self.semobj[nm] = nc.alloc_semaphore(nm)
            self.dq[q] = [names, 0]
        self.dlast = {}
        self.ninstr = 0

    def _deps(self, reads, writes):
        ev = {}

        def add(e):
            if e is not None and ev.get(e[0], 0) < e[1]:
                ev[e[0]] = e[1]
        for r in reads:
            st = self.regs.get(r)
            if st:
                add(st[0])
        for w in writes:
            st = self.regs.get(w)
            if st:
                add(st[0])
                for s, v in st[1].items():
                    add((s, v))
        return ev

    def _commit(self, reads, writes, e):
        for r in reads:
            st = self.regs.setdefault(r, [None, {}])
            if st[1].get(e[0], 0) < e[1]:
                st[1][e[0]] = e[1]
        for w in writes:
            self.regs[w] = [e, {}]

    def _emit_waits(self, X, ev, skip=None):
        for s, v in ev.items():
            if s == skip or self.waited[X].get(s, 0) >= v:
                continue
            self.waited[X][s] = v
            so = self.semobj[s]
            self.thunks[X].append(lambda e, so=so, v=v: e.wait_ge(so, v))
            self.ninstr += 1

    def op(self, X, fn, reads=(), writes=(), inc=True):
        rec = _Rec()
        fn(rec)
        name_, a_, k_ = rec.call
        fn = lambda eng, name_=name_, a_=a_, k_=k_: getattr(eng, name_)(*a_, **k_)
        ev = self._deps(reads, writes)
        own = self.own[X]
        self._emit_waits(X, ev, skip=own if X == 'pe' else None)
        e = (own, self.cnt[X] + 1)
        if inc:
            self.cnt[X] += 1
            so = self.sem[X]
            self.thunks[X].append(lambda eng, fn=fn, so=so: fn(eng).then_inc(so, 1))
        else:
            self.thunks[X].append(lambda eng, fn=fn: fn(eng))
        self.ninstr += 1
        self._commit(reads, writes, e)

    def dma(self, Q, out, in_, reads=(), writes=(), **kw):
        ev = self._deps(reads, writes)
        names, j = self.dq[Q]
        self.dq[Q][1] += 1
        K = len(names)
        name = names[j % K]
        if j >= K and ev.get(name, 0) < 16 * (j // K):
            ev[name] = 16 * (j // K)
        self._emit_waits(Q, ev)
        e = (name, 16 * (j // K + 1))
        self.dlast[name] = e[1]
        so = self.semobj[name]
        self.thunks[Q].append(lambda eng, so=so, out=out, in_=in_, kw=kw: eng.dma_start(out=out, in_=in_, **kw).then_inc(so, 16))
        self.ninstr += 1
        self._commit(reads, writes, e)
        return e

    def barrier(self):
        ev = dict(self.dlast)
        for k in ('pe', 'act', 'dve', 'pool'):
            if self.cnt[k] > 0:
                ev[self.own[k]] = self.cnt[k]
        for X in self.ENG:
            self._emit_waits(X, ev, skip=self.own.get(X))
        self.regs = {}
        for k in ('pe', 'act', 'dve', 'pool'):
            self.semval[self.own[k]] = self.cnt[k]
        self.set = (self.set + 1) % self.NSET
        for k in ('pe', 'act', 'dve', 'pool'):
            self.own[k] = 's_%s_%d' % (k, self.set)
            self.sem[k] = self.semobj[self.own[k]]
            self.cnt[k] = self.semval[self.own[k]]

    def finish(self):
        self.barrier()
        nc = self.nc
        th = self.thunks
        with nc.Block() as block:
            @block.tensor
            def _(e):
                for t in th['pe']:
                    t(e)

            @block.scalar
            def _(e):
                for t in th['act']:
                    t(e)

            @block.vector
            def _(e):
                for t in th['dve']:
                    t(e)

            @block.gpsimd
            def _(e):
                for t in th['pool']:
                    t(e)

            @block.sync
            def _(e):
                for t in th['sp']:
                    t(e)


class Arena:
    def __init__(self, ap, ncols):
        self.ap = ap
        self.n = ncols
        self.n0 = ncols
        self.off = 0

    def reset(self, reserve=0):
        self.off = 0
        self.n = self.n0 - reserve

    def f32(self, cols):
        cols_al = (cols + 7) // 8 * 8
        assert self.off + cols_al <= self.n, ('arena overflow', self.off, cols_al, self.n)
        v = self.ap[:, self.off:self.off + cols]
        self.off += cols_al
        return v

    def bf16(self, cols):
        c32 = (cols + 1) // 2
        c32 = (c32 + 7) // 8 * 8
        assert self.off + c32 <= self.n, ('arena overflow', self.off, c32, self.n)
        v = self.ap[:, self.off:self.off + c32].bitcast(BF16)[:, 0:cols]
        self.off += c32
        return v


def bc(ap, shape):
    return ap.to_broadcast(list(shape))


def build(NL=4, NB=2, dbg=False, stop_after=None):
    nc = bass.Bass("TRN2", target_bir_lowering=False)

    def din(name, shape, dt=F32):
        return nc.dram_tensor(name, list(shape), dt, kind="ExternalInput").ap()

    def dscr(name, shape, dt=F32):
        kind = "ExternalOutput" if dbg else "Internal"
        return nc.dram_tensor(name, list(shape), dt, kind=kind).ap()

    x = din('x', [NB, 4096, D])
    cc = din('c', [NB, D])
    ctx = din('ctx', [NB, 256, D])
    c_ctx = din('c_ctx', [1, D])
    ada_w = din('ada_w', [4, D, 6 * D])
    ada_b = din('ada_b', [4, 1, 6 * D])
    nrm = din('norms', [4, 4, D])
    w_in = din('w_in', [4, D, 3344])
    ssd_cw = din('ssd_cw', [4, 4, 768])
    ssd_dtb = din('ssd_dtb', [4, 1, 16])
    ssd_alog = din('ssd_alog', [4, 1, 16])
    ssd_d = din('ssd_d', [4, 1, 8])
    ssd_norm = din('ssd_norm', [4, 1, 512])
    rpbT = din('rpbT', [4, 2, 128, 960])
    namask = din('namask', [128, 64])
    hg_lbl = din('hg_lbl', [4, 256])
    hg_norm = din('hg_norm', [4, 1, 64])
    w_out = din('w_out', [4, D, D])
    w_up = din('w_up', [4, D, 2 * DFF])
    ffn_cw = din('ffn_cw', [4, 4, DFF])
    w_down = din('w_down', [4, DFF, D])
    y = nc.dram_tensor('y', [NB, 4096, D], F32, kind="ExternalOutput").ap()

    Hs = dscr('Hs', [NB, T, D])
    UT = dscr('UT', [NB, D, UTW], BF16)
    XB = dscr('XB', [NB, T, 640], BF16)
    BCs = dscr('BCs', [NB, 2, 128, T], BF16)
    TOK = dscr('TOK', [NB, T, 784])
    TB = dscr('TB', [NB, T, 512], BF16)
    NAQK = dscr('NAQK', [NB, 4, 128, T], BF16)
    HGQK = dscr('HGQK', [NB, 2, 2, 2, 128, T], BF16)
    HGS = dscr('HGS', [NB, 2, 2, 128, 136, 3])
    YF = dscr('YF', [NB, T, 512])
    OF = dscr('OF', [NB, T, 256])
    MIX = dscr('MIX', [NB, T, D], BF16)
    DBG = dscr('DBG', [128, 128])
    DBGW = dscr('DBGW', [128, 22 * D], BF16)
    DBGU = dscr('DBGU', [128, 8 * 2 * DFF], BF16)

    S = Sched(nc)
    sb = nc.alloc_sbuf_tensor
    ident = sb('ident', [128, 128], BF16)
    ones = sb('ones', [128, 512], F32)
    TRIf = sb('TRIf', [128, 128], F32)
    TRIb = sb('TRIb', [128, 128], F32)
    MGT = sb('MGT', [128, 128], F32)
    MLT = sb('MLT', [128, 128], F32)
    BMf = sb('BMf', [128, 128], F32)
    BMb = sb('BMb', [128, 128], F32)
    zer = sb('zer', [128, 64], BF16)
    RT = sb('RT', [4, 128], F32)
    I4 = sb('I4', [4, 4], F32)
    RM = sb('RM', [128, 4], F32)
    sT = sb('sT', [128, 3, 8], F32)
    LB = sb('LB', [128, 2, 4], F32)
    OML = sb('OML', [128, 2, 4], F32)
    NOML = sb('NOML', [128, 2, 4], F32)
    MSg = sb('MSg', [128, 3, D], F32)
    dtb = sb('dtb', [128, 16], F32)
    aneg = sb('aneg', [128, 16], F32)
    dsk = sb('dsk', [128, 8], F32)
    ssdn = sb('ssdn', [128, 512], F32)
    hgn = sb('hgn', [128, 64], F32)
    scw = sb('scw', [128, 6, 4], F32)
    fcw = sb('fcw', [128, 22, 4], F32)
    TBL = sb('TBL', [128, 2, 960], F32)
    epsc = sb('epsc', [128, 1], F32)
    AW = (nc.sbuf_bytes_remaining - 2048) // 4 // 8 * 8
    arena_t = sb('arena', [128, AW], F32)
    A = Arena(arena_t, AW)
    MSA_N = 3 * 2 * D
    MSa = arena_t[:, AW - MSA_N:AW].rearrange("p (j a c) -> p j a c", j=3, a=2)
    PS = [nc.alloc_psum_tensor('ps%d' % i, [128, 1024], F32) for i in range(4)]

    def pool_const():
        S.op('pool', lambda e: e.memset(ident[:], 1.0), writes=['c'])
        S.op('pool', lambda e: e.affine_select(ident[:], ident[:], [[-1, 128]], ALU.is_equal, 0.0, base=0, channel_multiplier=1), writes=['c'])
        S.op('pool', lambda e: e.memset(ones[:], 1.0), writes=['c'])
        S.op('pool', lambda e: e.memset(zer[:], 0.0), writes=['c'])
        S.op('pool', lambda e: e.memset(epsc[:], EPS), writes=['c'])
        for m, pat, cm, op in ((TRIf, 1, -1, ALU.is_ge), (TRIb, -1, 1, ALU.is_ge), (MGT, -1, 1, ALU.is_gt), (MLT, 1, -1, ALU.is_gt),
                               (BMf, 1, -1, ALU.is_ge), (BMb, -1, 1, ALU.is_ge)):
            S.op('pool', lambda e, m=m: e.memset(m[:], 1.0), writes=['c'])
            S.op('pool', lambda e, m=m, pat=pat, cm=cm, op=op: e.affine_select(m[:], m[:], [[pat, 128]], op, 0.0, base=0, channel_multiplier=cm), writes=['c'])
        S.op('pool', lambda e: e.memset(RT[:], 1.0), writes=['c'])
        S.op('pool', lambda e: e.affine_select(RT[:], RT[:], [[1, 128]], ALU.is_ge, 0.0, base=0, channel_multiplier=-32), writes=['c'])
        S.op('pool', lambda e: e.affine_select(RT[:], RT[:], [[-1, 128]], ALU.is_ge, 0.0, base=31, channel_multiplier=32), writes=['c'])
        S.op('pool', lambda e: e.memset(I4[:], 1.0), writes=['c'])
        S.op('pool', lambda e: e.affine_select(I4[:], I4[:], [[-1, 4]], ALU.is_equal, 0.0, base=0, channel_multiplier=1), writes=['c'])
        S.op('pe', lambda e: e.matmul(PS[0][:, 0:128], RT[:], RT[:], start=True, stop=True), reads=['c'], writes=['cps'], inc=False)
        S.op('pe', lambda e: e.matmul(PS[0][:, 128:132], RT[:], I4[:], start=True, stop=True), reads=['c'], writes=['cps'])
        S.op('dve', lambda e: e.tensor_tensor(BMf[:], BMf[:], PS[0][:, 0:128], ALU.mult), reads=['c', 'cps'], writes=['c'])
        S.op('dve', lambda e: e.tensor_tensor(BMb[:], BMb[:], PS[0][:, 0:128], ALU.mult), reads=['c', 'cps'], writes=['c'])
        S.op('dve', lambda e: e.tensor_copy(RM[:], PS[0][:, 128:132]), reads=['cps'], writes=['c'])

    pool_const()

    def row_bcast(dst, src_row_dram, n, scratch_row, psv, post=None):
        S.dma('sp', scratch_row[0:1, 0:n], src_row_dram, writes=['rb_row'])
        for o in range(0, n, 512):
            w = min(512, n - o)
            S.op('pe', lambda e, o=o, w=w: e.matmul(psv[:, 0:w], ones[0:1, 0:128], scratch_row[0:1, o:o + w], start=True, stop=True), reads=['rb_row', 'c'], writes=['rb_ps'])
            if post is None:
                S.op('dve', lambda e, o=o, w=w: e.tensor_copy(dst[:, o:o + w], psv[:, 0:w]), reads=['rb_ps'], writes=['rb_dst'])
            else:
                post(o, w)

    def load_cast(dst, src, nk, cols, stg, tag, smax=4096):
        step = max(1, min(cols, smax // nk))
        i = 0
        for o in range(0, cols, step):
            w = min(step, cols - o)
            st = stg[i % 2]
            for k0 in range(0, nk, 8):
                k1 = min(nk, k0 + 8)
                S.dma('sp', st[:, 0:nk * w].rearrange("p (k c) -> p k c", c=w)[:, k0:k1, :], src[:, k0:k1, o:o + w], writes=['stg%d' % (i % 2)])
            eng = 'pool' if i % 2 == 0 else 'act'
            if eng == 'pool':
                S.op('pool', lambda e, st=st, o=o, w=w: e.tensor_copy(dst[:, :, o:o + w], st[:, 0:nk * w].rearrange("p (k c) -> p k c", c=w)), reads=['stg%d' % (i % 2)], writes=[tag])
            else:
                S.op('act', lambda e, st=st, o=o, w=w: e.activation(dst[:, :, o:o + w], st[:, 0:nk * w].rearrange("p (k c) -> p k c", c=w), AF.Copy), reads=['stg%d' % (i % 2)], writes=[tag])
            i += 1

    def rstd_from_ss(ss, rs, n, tagr, tagw):
        S.op('act', lambda e: e.activation(rs, ss, AF.Sqrt, bias=epsc[:, 0:1], scale=1.0 / n), reads=tagr + ['c'], writes=tagw)
        S.op('dve', lambda e: e.reciprocal(rs, rs), reads=tagw, writes=tagw)

    def p_init():
        A.reset()
        for b in range(NB):
            S.dma('sp', Hs[b, 0:256, :], ctx[b])
            for r in range(0, 4096, 1024):
                S.dma('sp', Hs[b, 256 + r:256 + r + 1024, :], x[b, r:r + 1024, :])
            for col in (0, 257, UTW - 1):
                S.dma('sp', UT[b].rearrange("(k p) c -> p k c", p=128)[:, :, col:col + 1], zer[:, 0:8].rearrange("p (k o) -> p k o", o=1), reads=['c'], allow_slow_non_contiguous=True)
        for j in range(3):
            src = cc[j:j + 1, :] if j < 2 and j < NB else (c_ctx if j == 2 else cc[0:1, :])
            S.dma('sp', sT[:, j, :], src.rearrange("o (k p) -> p (o k)", p=128), writes=['sT'], allow_slow_non_contiguous=True)
        S.op('act', lambda e: e.activation(sT[:], sT[:], AF.Silu), reads=['sT'], writes=['sT'])
        ex = A.f32(8)
        tot = A.f32(2)
        for c_ in range(2):
            for l_ in range(4):
                S.dma('sp', ex[:, c_ * 4 + l_:c_ * 4 + l_ + 1], hg_lbl[l_:l_ + 1, c_ * 128:(c_ + 1) * 128].rearrange("o p -> p o"), writes=['ex'], allow_slow_non_contiguous=True)
        S.op('act', lambda e: e.activation(ex, ex, AF.Exp), reads=['ex'], writes=['ex'])
        exv = ex.rearrange("p (c l) -> p c l", l=4)
        S.op('dve', lambda e: e.tensor_reduce(tot, exv, AX.X, ALU.add), reads=['ex'], writes=['tot'])
        S.op('dve', lambda e: e.reciprocal(tot, tot), reads=['tot'], writes=['tot'])
        S.op('dve', lambda e: e.memset(LB[:], 0.0), writes=['LB'])
        for l in range(1, 4):
            S.op('dve', lambda e, l=l: e.tensor_tensor(LB[:, :, l], LB[:, :, l - 1], exv[:, :, l], ALU.add), reads=['LB', 'ex'], writes=['LB'])
        S.op('dve', lambda e: e.tensor_tensor(LB[:], LB[:], bc(tot.unsqueeze(2), [128, 2, 4]), ALU.mult), reads=['LB', 'tot'], writes=['LB'])
        S.op('dve', lambda e: e.tensor_scalar(OML[:], LB[:], -1.0, 1.0, ALU.mult, ALU.add), reads=['LB'], writes=['OML'])
        S.op('dve', lambda e: e.tensor_scalar(NOML[:], OML[:], -1.0, None, ALU.mult), reads=['OML'], writes=['NOML'])
        S.barrier()

    def p_ada(l, half):
        A.reset(MSA_N)
        srep = A.f32(3 * 8 * 128).rearrange("p (j k c) -> p j k c", j=3, k=8)
        awb = [A.f32(8 * 512) for _ in range(2)]
        abr = [A.f32(512) for _ in range(2)]
        nrow = A.f32(2 * D).rearrange("p (a c) -> p a c", a=2)
        rowt = A.f32(D)
        for j in range(3):
            S.op('dve', lambda e, j=j: e.tensor_copy(srep[:, j], bc(sT[:, j, :].unsqueeze(2), [128, 8, 128])), reads=['sT'], writes=['srep'])
        for a in range(2):
            row_bcast(nrow[:, a, :], nrm[l, 2 * half + a:2 * half + a + 1, :], D, rowt, PS[3][:, 0:512])
        for cgi in range(6):
            cg = half * 6 + cgi
            kind, hh = cgi // 2, cgi % 2
            aw = awb[cgi % 2]
            ab = abr[cgi % 2]
            awv = aw.rearrange("p (k c) -> p k c", c=512)
            S.dma('sp', awv, ada_w[l].rearrange("(k p) c -> p k c", p=128)[:, :, cg * 512:(cg + 1) * 512], writes=['aw%d' % (cgi % 2)])
            S.dma('sp', ab[0:1, :], ada_b[l, :, cg * 512:(cg + 1) * 512], writes=['ab%d' % (cgi % 2)])
            for j in range(3):
                pv = PS[j % 2][:, 0:512]
                pt = 'adaps%d' % (j % 2)
                for k in range(8):
                    S.op('pe', lambda e, j=j, k=k, pv=pv, awv=awv: e.matmul(pv, srep[:, j, k, :], awv[:, k, :], start=(k == 0), stop=False), reads=['srep', 'aw%d' % (cgi % 2)], writes=[pt], inc=False)
                S.op('pe', lambda e, pv=pv, ab=ab: e.matmul(pv, ones[0:1, 0:128], ab[0:1, :], start=False, stop=True), reads=['ab%d' % (cgi % 2), 'c'], writes=[pt])
                cs = slice(hh * 512, hh * 512 + 512)
                if kind == 0:
                    S.op('dve', lambda e, j=j, pv=pv, cs=cs: e.tensor_copy(MSa[:, j, 1, cs], pv), reads=[pt], writes=['MS'])
                elif kind == 1:
                    S.op('dve', lambda e, j=j, pv=pv, cs=cs: e.scalar_tensor_tensor(MSa[:, j, 0, cs], pv, 1.0, nrow[:, 0, cs], ALU.add, ALU.mult), reads=[pt, 'rb_dst'], writes=['MS'])
                else:
                    S.op('dve', lambda e, j=j, pv=pv, cs=cs: e.tensor_tensor(MSg[:, j, cs], pv, nrow[:, 1, cs], ALU.mult), reads=[pt, 'rb_dst'], writes=['MS'])
        S.barrier()

    def p_small(l):
        A.reset()
        rowt = A.f32(512)
        pv = PS[3][:, 0:512]
        row_bcast(dtb[:], ssd_dtb[l], 16, rowt, pv)
        row_bcast(aneg[:], ssd_alog[l], 16, rowt, pv)
        S.op('act', lambda e: e.activation(aneg[:], aneg[:], AF.Exp), reads=['rb_dst'], writes=['rb_dst'])
        S.op('dve', lambda e: e.tensor_scalar(aneg[:], aneg[:], -1.0, None, ALU.mult), reads=['rb_dst'], writes=['rb_dst'])
        row_bcast(dsk[:], ssd_d[l], 8, rowt, pv)
        row_bcast(ssdn[:], ssd_norm[l], 512, rowt, pv)
        row_bcast(hgn[:], hg_norm[l], 64, rowt, pv)
        for w_ in range(4):
            S.dma('sp', scw[:, :, w_], ssd_cw[l, w_:w_ + 1, :].rearrange("o (k p) -> p (o k)", p=128), writes=['scw'], allow_slow_non_contiguous=True)
            S.dma('sp', fcw[:, :, w_], ffn_cw[l, w_:w_ + 1, :].rearrange("o (k p) -> p (o k)", p=128), writes=['fcw'], allow_slow_non_contiguous=True)
        mk = A.f32(64)
        S.dma('sp', mk, namask, writes=['mk'])
        for p in range(2):
            S.dma('sp', TBL[:, p, :], rpbT[l, p], writes=['TBL'])
            S.op('dve', lambda e, p=p: e.tensor_tensor(TBL[:, p, :].rearrange("q (r c) -> q r c", c=64), TBL[:, p, :].rearrange("q (r c) -> q r c", c=64),
                                                      bc(mk.unsqueeze(1), [128, 15, 64]), ALU.add), reads=['TBL', 'mk'], writes=['TBL'])
        if dbg:
            S.dma('pool', DBG[:, 0:88], fcw[:].rearrange("p k w -> p (k w)"), reads=['fcw'])
            S.dma('pool', DBG[:, 88:112], scw[:].rearrange("p k w -> p (k w)"), reads=['scw'])
        S.barrier()

    def p_normT(l, tiles=None):
        A.reset(MSA_N)
        hb = [A.f32(D) for _ in range(2)]
        tmp = A.f32(D)
        sq = A.f32(D)
        ub = [A.bf16(D) for _ in range(2)]
        utg = [A.bf16(8 * 512) for _ in range(2)]
        st = A.f32(4)
        gi = 0
        for b in range(NB):
            groups = [(0, 2)] + [(2 + 4 * m, 4) for m in range(8)]
            for (t0, nt) in groups:
                ug = utg[gi % 2]
                ugv = ug.rearrange("p (k c) -> p k c", k=8)
                for ti in range(nt):
                    j = t0 + ti
                    jj = b * 34 + j
                    mj = 2 if j < 2 else b
                    h = hb[jj % 2]
                    u = ub[jj % 2]
                    ht, utg_t = 'h%d' % (jj % 2), 'u%d' % (jj % 2)
                    S.dma('sp', h, Hs[b, j * 128:(j + 1) * 128, :], writes=[ht])
                    S.op('act', lambda e, h=h: e.activation(sq, h, AF.Square, accum_out=st[:, 0:1]), reads=[ht], writes=['sq', 'st'])
                    rstd_from_ss(st[:, 0:1], st[:, 1:2], D, ['st'], ['st1'])
                    S.op('dve', lambda e, h=h, mj=mj: e.scalar_tensor_tensor(tmp, h, st[:, 1:2], MSa[:, mj, 0, :], ALU.mult, ALU.mult), reads=[ht, 'st1'], writes=['tmp'])
                    S.op('pool', lambda e, u=u, mj=mj: e.tensor_tensor(u, tmp, MSa[:, mj, 1, :], ALU.add), reads=['tmp'], writes=[utg_t])
                    pv = PS[jj % 2].bitcast(BF16)[:, 0:1024].rearrange("p (k c) -> p k c", k=8)
                    pt = 'tp%d' % (jj % 2)
                    for k in range(8):
                        S.op('pe', lambda e, pv=pv, u=u, k=k: e.transpose(pv[:, k, :], u[:, k * 128:(k + 1) * 128], ident[:]), reads=[utg_t, 'c'], writes=[pt], inc=(k == 7))
                    S.op('act', lambda e, pv=pv, ugv=ugv, ti=ti: e.activation(ugv[:, :, ti * 128:(ti + 1) * 128], pv, AF.Copy), reads=[pt], writes=['ug%d' % (gi % 2)])
                n = nt * 128
                c0 = ucol(t0 * 128)
                S.dma('pool', UT[b].rearrange("(k p) c -> p k c", p=128)[:, :, c0:c0 + n], ugv[:, :, 0:n], reads=['ug%d' % (gi % 2)])
                gi += 1
        S.barrier()

    WIN_MAP = [(0, 512, 768), (768, 1296, 256), (1024, 1552, 256), (1280, 2064, 256), (1536, 2320, 512),
               (2048, 0, 512), (2560, 3088, 256), (2816, 1280, 16), (2832, 1808, 256), (3088, 2832, 256)]

    def p_inproj(l):
        A.reset()
        win = A.bf16(8 * 3344).rearrange("p (k c) -> p k c", k=8)
        stg = [A.f32(4096) for _ in range(2)]
        wsrc = w_in[l].rearrange("(k p) c -> p k c", p=128)
        for (dc, sc, w) in WIN_MAP:
            load_cast(win[:, :, dc:dc + w], wsrc[:, :, sc:sc + w], 8, w, stg, 'win')
        S.barrier()
        A.off -= 2 * 4096
        uts = [A.bf16(8 * 514) for _ in range(2)]
        xbcT = A.bf16(6 * 512).rearrange("p (k c) -> p k c", k=6)
        ctmp = A.f32(512)
        xtok = [A.bf16(640) for _ in range(2)]
        tokf = [A.f32(784) for _ in range(2)]
        tokb = [A.bf16(512) for _ in range(2)]
        nqk = [A.bf16(512) for _ in range(2)]
        qs = A.f32(2 * 512).rearrange("p (k c) -> p k c", k=2)
        sig = A.f32(512)
        t1 = A.f32(512)
        kk = A.f32(512)
        aext = A.f32(520)
        dd = A.f32(512)
        eD = A.f32(512)
        hqk = [A.bf16(2 * 512) for _ in range(2)]
        sc3 = [A.f32(48) for _ in range(2)]
        sc3t = A.f32(48)
        cnt = {'ps': 0, 'x': 0, 't': 0, 'n': 0, 'h': 0}

        def nps():
            i = cnt['ps'] % 4
            cnt['ps'] += 1
            return PS[i], 'ps%d' % i

        si = 0
        for b in range(NB):
            sts = [(0, 256)] + [(256 + 512 * m, 512) for m in range(8)]
            for (p0, n) in sts:
                ut = uts[si % 2]
                utv = ut.rearrange("p (k c) -> p k c", k=8)
                utag = 'ut%d' % (si % 2)
                si += 1
                c0 = ucol(p0)
                S.dma('sp', utv[:, :, 0:n + 2], UT[b].rearrange("(k p) c -> p k c", p=128)[:, :, c0 - 1:c0 + n + 1], writes=[utag])
                npc = n // 256
                for ch in range(6):
                    ps, pt = nps()
                    for pc in range(npc):
                        for k in range(8):
                            S.op('pe', lambda e, ps=ps, pc=pc, k=k, ch=ch, utv=utv: e.matmul(ps[:, pc * 512:pc * 512 + 258], win[:, k, ch * 128:(ch + 1) * 128], utv[:, k, pc * 256:pc * 256 + 258], start=(k == 0), stop=(k == 7)),
                                 reads=['win', utag], writes=[pt], inc=(k == 7 and pc == npc - 1))
                    pv = ps.rearrange("p (a c) -> p a c", c=512)[:, 0:npc, :]
                    cv = ctmp[:, 0:n].rearrange("p (a c) -> p a c", c=256)
                    S.op('act', lambda e, pv=pv, cv=cv, ch=ch: e.activation(cv, pv[:, :, 0:256], AF.Copy, scale=scw[:, ch, 0:1]), reads=[pt, 'scw'], writes=['ctmp'])
                    S.op('dve', lambda e, pv=pv, cv=cv, ch=ch: e.scalar_tensor_tensor(cv, pv[:, :, 1:257], scw[:, ch, 1:2], cv, ALU.mult, ALU.add), reads=[pt, 'ctmp'], writes=['ctmp'])
                    S.op('dve', lambda e, pv=pv, cv=cv, ch=ch: e.scalar_tensor_tensor(cv, pv[:, :, 2:258], scw[:, ch, 2:3], cv, ALU.mult, ALU.add), reads=[pt, 'ctmp'], writes=['ctmp'])
                    S.op('act', lambda e, ch=ch, n=n: e.activation(xbcT[:, ch, 0:n], ctmp[:, 0:n], AF.Silu, bias=scw[:, ch, 3:4]), reads=['ctmp'], writes=['xbcT%d' % ch])
                S.dma('pool', BCs[b].rearrange("a p t -> p a t")[:, :, p0:p0 + n], xbcT[:, 4:6, 0:n], reads=['xbcT4', 'xbcT5'])
                for s_ in range(n // 128):
                    pos = p0 + s_ * 128
                    ps, pt = nps()
                    pvb = ps.bitcast(BF16)[:, 0:640].rearrange("p (k c) -> p k c", k=5)
                    for k in range(5):
                        S.op('pe', lambda e, pvb=pvb, k=k, s_=s_: e.transpose(pvb[:, k, :], xbcT[:, k, s_ * 128:(s_ + 1) * 128], ident[:]), reads=['xbcT%d' % k, 'c'], writes=[pt], inc=(k == 4))
                    xt = xtok[cnt['x'] % 2]
                    xtag = 'xtok%d' % (cnt['x'] % 2)
                    cnt['x'] += 1
                    S.op('act', lambda e, xt=xt, ps=ps: e.activation(xt, ps.bitcast(BF16)[:, 0:640], AF.Copy), reads=[pt], writes=[xtag])
                    S.dma('pool', XB[b, pos:pos + 128, :], xt, reads=[xtag])
                    tf = tokf[cnt['t'] % 2]
                    tb_ = tokb[cnt['t'] % 2]
                    ttag = 'tok%d' % (cnt['t'] % 2)
                    cnt['t'] += 1
                    lhs = lambda k, s_=s_, utv=utv: utv[:, k, 1 + s_ * 128:1 + (s_ + 1) * 128]
                    for (wc, w, dst, dtag) in ((2048, 512, tf[:, 0:512], ttag + 'f'), (2560, 272, tf[:, 512:784], ttag + 'f'), (2832, 512, tb_, ttag + 'b')):
                        ps, pt = nps()
                        for k in range(8):
                            S.op('pe', lambda e, ps=ps, k=k, wc=wc, w=w, lhs=lhs: e.matmul(ps[:, 0:w], lhs(k), win[:, k, wc:wc + w], start=(k == 0), stop=(k == 7)), reads=['win', utag], writes=[pt], inc=(k == 7))
                        S.op('act', lambda e, ps=ps, w=w, dst=dst: e.activation(dst, ps[:, 0:w], AF.Copy), reads=[pt], writes=[dtag])
                    S.dma('pool', TOK[b, pos:pos + 128, :], tf, reads=[ttag + 'f'])
                    S.dma('pool', TB[b, pos:pos + 128, :], tb_, reads=[ttag + 'b'])
                for ch in range(4):
                    ps, pt = nps()
                    for k in range(8):
                        S.op('pe', lambda e, ps=ps, k=k, ch=ch, utv=utv, n=n: e.matmul(ps[:, 0:n], win[:, k, 768 + ch * 128:768 + (ch + 1) * 128], utv[:, k, 1:1 + n], start=(k == 0), stop=(k == 7)), reads=['win', utag], writes=[pt], inc=(k == 7))
                    nb_ = nqk[cnt['n'] % 2]
                    ntag = 'nqk%d' % (cnt['n'] % 2)
                    cnt['n'] += 1
                    S.op('act', lambda e, ps=ps, nb_=nb_, n=n, ch=ch: e.activation(nb_[:, 0:n], ps[:, 0:n], AF.Copy, scale=(0.125 if ch < 2 else 1.0)), reads=[pt], writes=[ntag])
                    S.dma('pool', NAQK[b, ch, :, p0:p0 + n], nb_[:, 0:n], reads=[ntag])
                for ch in range(2):
                    ps, pt = nps()
                    for k in range(8):
                        S.op('pe', lambda e, ps=ps, k=k, ch=ch, utv=utv, n=n: e.matmul(ps[:, 0:n], win[:, k, 1280 + ch * 128:1280 + (ch + 1) * 128], utv[:, k, 1:1 + n], start=(k == 0), stop=(k == 7)), reads=['win', utag], writes=[pt], inc=(k == 7))
                    S.op('act', lambda e, ps=ps, ch=ch, n=n: e.activation(qs[:, ch, 0:n], ps[:, 0:n], AF.Silu), reads=[pt], writes=['qs%d' % ch])
                nch = n // 32
                for d_ in range(2):
                    for pr in range(2):
                        ps, pt = nps()
                        fc = 1536 + (d_ * 2 + pr) * 128
                        for k in range(8):
                            S.op('pe', lambda e, ps=ps, k=k, fc=fc, utv=utv, n=n: e.matmul(ps[:, 0:n], win[:, k, fc:fc + 128], utv[:, k, 1:1 + n], start=(k == 0), stop=(k == 7)), reads=['win', utag], writes=[pt], inc=(k == 7))
                        S.op('act', lambda e, ps=ps, n=n: e.activation(sig[:, 0:n], ps[:, 0:n], AF.Sigmoid), reads=[pt], writes=['sig'])
                        S.op('dve', lambda e, n=n, pr=pr: e.tensor_scalar(t1[:, 0:n], sig[:, 0:n], OML[:, pr, l:l + 1], LB[:, pr, l:l + 1], ALU.mult, ALU.add), reads=['sig'], writes=['t1'])
                        S.op('act', lambda e, n=n: e.activation(t1[:, 0:n], t1[:, 0:n], AF.Ln), reads=['t1'], writes=['t1'])
                        S.op('dve', lambda e, n=n, pr=pr: e.tensor_scalar(kk[:, 0:n], sig[:, 0:n], NOML[:, pr, l:l + 1], OML[:, pr, l:l + 1], ALU.mult, ALU.add), reads=['sig'], writes=['kk'])
                        S.op('dve', lambda e: e.memset(aext[:, 0:1], 0.0), writes=['aext'])
                        S.op('dve', lambda e, n=n: e.tensor_tensor_scan(aext[:, 1:n + 1], ones[:, 0:n], t1[:, 0:n], 0.0, ALU.mult, ALU.add), reads=['t1', 'c'], writes=['aext'])
                        off = 1 if d_ == 0 else 0
                        Pv = aext[:, off:off + n].rearrange("p (c t) -> p c t", t=32)
                        pref = Pv[:, :, 16:17]
                        ddv = dd[:, 0:n].rearrange("p (c t) -> p c t", t=32)
                        if d_ == 0:
                            S.op('dve', lambda e, Pv=Pv, pref=pref, ddv=ddv, nch=nch: e.tensor_tensor(ddv, Pv, bc(pref, [128, nch, 32]), ALU.subtract), reads=['aext'], writes=['dd'])
                        else:
                            S.op('dve', lambda e, Pv=Pv, pref=pref, ddv=ddv, nch=nch: e.tensor_tensor(ddv, bc(pref, [128, nch, 32]), Pv, ALU.subtract), reads=['aext'], writes=['dd'])
                        hb_ = hqk[cnt['h'] % 2]
                        hbv = hb_.rearrange("p (a c) -> p a c", a=2)
                        s3 = sc3[cnt['h'] % 2]
                        htag = 'hqk%d' % (cnt['h'] % 2)
                        cnt['h'] += 1
                        S.op('act', lambda e, n=n: e.activation(eD[:, 0:n], dd[:, 0:n], AF.Exp), reads=['dd'], writes=['eD'])
                        S.op('dve', lambda e, n=n, pr=pr, hbv=hbv: e.tensor_tensor(hbv[:, 0, 0:n], qs[:, pr, 0:n], eD[:, 0:n], ALU.mult), reads=['eD', 'qs%d' % pr], writes=[htag])
                        S.op('act', lambda e, n=n: e.activation(eD[:, 0:n], dd[:, 0:n], AF.Exp, scale=-1.0), reads=['dd', htag], writes=['eD'])
                        S.op('dve', lambda e, n=n, hbv=hbv: e.tensor_tensor(hbv[:, 1, 0:n], kk[:, 0:n], eD[:, 0:n], ALU.mult), reads=['eD', 'kk'], writes=[htag])
                        lo = aext[:, 0:n].rearrange("p (c t) -> p c t", t=32)[:, :, 0:1]
                        hi = aext[:, 1:n + 1].rearrange("p (c t) -> p c t", t=32)[:, :, 31:32]
                        s3v = sc3t[:, 0:nch * 3].rearrange("p (c t) -> p c t", t=3)
                        if d_ == 0:
                            trip = ((pref, lo), (hi, lo), (hi, pref))
                        else:
                            trip = ((hi, pref), (hi, lo), (pref, lo))
                        for i3, (a_, b_) in enumerate(trip):
                            S.op('dve', lambda e, i3=i3, a_=a_, b_=b_, s3v=s3v: e.tensor_tensor(s3v[:, :, i3:i3 + 1], a_, b_, ALU.subtract), reads=['aext'], writes=['sc3t'])
                        S.op('act', lambda e, s3=s3, nch=nch: e.activation(s3[:, 0:nch * 3], sc3t[:, 0:nch * 3], AF.Exp), reads=['sc3t'], writes=[htag + 's'])
                        S.dma('pool', HGQK[b, d_, :, pr, :, p0:p0 + n].rearrange("a p t -> p a t"), hbv[:, :, 0:n], reads=[htag])
                        S.dma('pool', HGS[b, d_, pr, :, p0 // 32:p0 // 32 + nch, :], s3[:, 0:nch * 3].rearrange("p (c t) -> p c t", t=3), reads=[htag + 's'])
        S.barrier()

    def p_ssd(l, d_):
        import os
        STG = int(os.environ.get('K_SSD_STAGE', '99'))
        NCH = int(os.environ.get('K_SSD_NCH', '99'))
        A.reset()
        TRI = TRIf if d_ == 0 else TRIb
        MSK = MGT if d_ == 0 else MLT
        xb = [A.bf16(640) for _ in range(2)]
        bct = [A.bf16(256) for _ in range(2)]
        dtr = [A.f32(16) for _ in range(2)]
        zt = [A.f32(512) for _ in range(2)]
        yft = [A.f32(512) for _ in range(2)]
        sm = A.f32(64)
        xdt = A.bf16(512)
        xw = A.bf16(512)
        lh = A.f32(1024)
        Ee = A.f32(1024)
        scm = A.f32(256)
        MT = A.bf16(1024)
        tmpy = A.f32(512)
        yd = [A.f32(512) for _ in range(2)]
        Hm = A.f32(256)
        Hb = A.bf16(256)
        tmph = A.f32(256)
        sq = A.f32(512)
        st = A.f32(4)
        ob = [A.bf16(512) for _ in range(2)]
        Cm = [[A.bf16(128) for _ in range(2)] for _ in range(2)]
        for i in range(2):
            for g in range(2):
                S.op('pool', lambda e, i=i, g=g: e.memset(Cm[i][g], 0.0), writes=['Cm%d' % i])
        dt_, loga, cum, tot, ecum, wv, etot = [sm[:, i * 8:(i + 1) * 8] for i in range(7)]
        PSs, PSc, PSy, PSi = PS[0], PS[1], PS[2], PS[3]
        ci = 0
        for b in range(NB):
            S.op('dve', lambda e: e.memset(Hm, 0.0), reads=['Hb'], writes=['Hm'])
            S.op('pool', lambda e: e.memset(Hb, 0.0), reads=['Hm'], writes=['Hb'])
            order = list(range(34)) if d_ == 0 else [1, 0] + list(range(33, 1, -1))
            for j in order[:NCH]:
                pos = j * 128
                q = ci % 2
                ci += 1
                x_, bc_, dr = xb[q], bct[q].rearrange("p (a t) -> p a t", a=2), dtr[q]
                lt = 'ld%d' % q
                S.dma('sp', x_, XB[b, pos:pos + 128, :], writes=[lt + 'x'])
                S.dma('sp', bc_, BCs[b].rearrange("a p t -> p a t")[:, :, pos:pos + 128], writes=[lt + 'b'])
                S.dma('sp', dr, TOK[b, pos:pos + 128, 768:784], writes=[lt + 'd'])
                if d_ == 1:
                    S.dma('sp', zt[q], TOK[b, pos:pos + 128, 0:512], writes=[lt + 'z'])
                    S.dma('sp', yft[q], YF[b, pos:pos + 128, :], writes=[lt + 'y'])
                S.op('dve', lambda e, dr=dr: e.tensor_tensor(dt_, dr[:, d_ * 8:d_ * 8 + 8], dtb[:, d_ * 8:d_ * 8 + 8], ALU.add), reads=[lt + 'd'], writes=['dt'])
                S.op('act', lambda e: e.activation(dt_, dt_, AF.Exp), reads=['dt'], writes=['dt'])
                S.op('act', lambda e: e.activation(dt_, dt_, AF.Ln, bias=1.0), reads=['dt'], writes=['dt'])
                S.op('dve', lambda e: e.tensor_tensor(loga, dt_, aneg[:, d_ * 8:d_ * 8 + 8], ALU.mult), reads=['dt'], writes=['loga'])
                S.op('dve', lambda e, x_=x_: e.tensor_tensor(xdt.rearrange("p (h c) -> p h c", h=8), x_[:, 0:512].rearrange("p (h c) -> p h c", h=8), bc(dt_.unsqueeze(2), [128, 8, 64]), ALU.mult), reads=[lt + 'x', 'dt'], writes=['xdt'])
                if STG < 1:
                    continue
                S.op('pe', lambda e: e.matmul(PSc[:, 512:520], TRI[:], loga, start=True, stop=True), reads=['loga', 'c'], writes=['psc2'], inc=False)
                S.op('pe', lambda e: e.matmul(PSc[:, 520:528], ones[:, 0:128], loga, start=True, stop=True), reads=['loga', 'c'], writes=['psc2'])
                S.op('dve', lambda e: e.tensor_copy(sm[:, 16:32], PSc[:, 512:528]), reads=['psc2'], writes=['cum'])
                S.op('act', lambda e: e.activation(ecum, cum, AF.Exp), reads=['cum'], writes=['ecum'])
                S.op('dve', lambda e: e.tensor_tensor(wv, tot, cum, ALU.subtract), reads=['cum'], writes=['wv'])
                S.op('act', lambda e: e.activation(wv, wv, AF.Exp), reads=['wv'], writes=['wv'])
                S.op('act', lambda e: e.activation(etot, tot, AF.Exp), reads=['cum'], writes=['etot'])
                S.op('pool', lambda e: e.tensor_tensor(xw.rearrange("p (h c) -> p h c", h=8), xdt.rearrange("p (h c) -> p h c", h=8), bc(wv.unsqueeze(2), [128, 8, 64]), ALU.mult), reads=['xdt', 'wv'], writes=['xw'])
                if STG < 2:
                    continue
                lhv = lh.rearrange("p (h s) -> p h s", h=8)
                S.op('pool', lambda e: e.tensor_tensor(lhv, bc(MSK[:].unsqueeze(1), [128, 8, 128]), bc(loga.unsqueeze(2), [128, 8, 128]), ALU.mult), reads=['loga', 'c'], writes=['lh'])
                for h in range(8):
                    S.op('pe', lambda e, h=h: e.matmul(PSs[:, h * 128:(h + 1) * 128], lhv[:, h, :], TRI[:], start=True, stop=True), reads=['lh', 'c'], writes=['pss'], inc=(h == 7))
                for hf in range(2):
                    S.op('act', lambda e, hf=hf: e.activation(Ee[:, hf * 512:(hf + 1) * 512], PSs[:, hf * 512:(hf + 1) * 512], AF.Exp), reads=['pss'], writes=['Ee'])
                if STG < 3:
                    continue
                for g in range(2):
                    S.op('pool', lambda e, g=g, bc_=bc_: e.tensor_copy(Cm[q][g][g * 64:(g + 1) * 64, :], bc_[g * 64:(g + 1) * 64, 1, :]), reads=[lt + 'b'], writes=['Cm%d' % q])
                for g in range(2):
                    S.op('pe', lambda e, g=g, bc_=bc_: e.matmul(PSc[:, g * 128:(g + 1) * 128], bc_[:, 0, :], Cm[q][g], start=True, stop=True), reads=[lt + 'b', 'Cm%d' % q], writes=['psc'], inc=(g == 1))
                if os.environ.get('K_SUB', 'z') < 'b':
                    continue
                S.op('dve', lambda e: e.tensor_tensor(scm.rearrange("p (g t) -> p g t", g=2), PSc[:, 0:256].rearrange("p (g t) -> p g t", g=2), bc(TRI[:].unsqueeze(1), [128, 2, 128]), ALU.mult), reads=['psc', 'c'], writes=['scm'])
                S.op('dve', lambda e: e.tensor_tensor(MT.rearrange("p (g h t) -> p g h t", g=2, h=4), Ee.rearrange("p (g h t) -> p g h t", g=2, h=4),
                                                      bc(scm.rearrange("p (g t) -> p g t", g=2).unsqueeze(2), [128, 2, 4, 128]), ALU.mult), reads=['Ee', 'scm'], writes=['MT'])
                if os.environ.get('K_SUB', 'z') < 'c':
                    continue
                for h in range(8):
                    S.op('pe', lambda e, h=h: e.matmul(PSy[:, h * 64:(h + 1) * 64], MT[:, h * 128:(h + 1) * 128], xdt[:, h * 64:(h + 1) * 64], start=True, stop=True), reads=['MT', 'xdt'], writes=['psy'], inc=(h == 7))
                if STG < 4:
                    continue
                for g in range(2):
                    S.op('pe', lambda e, g=g: e.matmul(PSi[:, g * 256:(g + 1) * 256], Cm[q][g], Hb, start=True, stop=True), reads=['Cm%d' % q, 'Hb'], writes=['psi'], inc=(g == 1))
                ydq = yd[q]
                S.op('dve', lambda e: e.tensor_tensor(tmpy.rearrange("p (h c) -> p h c", h=8), PSi[:, 0:512].rearrange("p (h c) -> p h c", h=8), bc(ecum.unsqueeze(2), [128, 8, 64]), ALU.mult), reads=['psi', 'ecum'], writes=['tmpy'])
                S.op('dve', lambda e, ydq=ydq: e.tensor_tensor(ydq, tmpy, PSy[:, 0:512], ALU.add), reads=['tmpy', 'psy'], writes=['yd%d' % q])
                if STG < 5:
                    continue
                for g in range(2):
                    S.op('pe', lambda e, g=g, x_=x_: e.matmul(PSi[:, 512 + g * 256:512 + (g + 1) * 256], x_[:, 512:640], xw[:, g * 256:(g + 1) * 256], start=True, stop=True), reads=[lt + 'x', 'xw'], writes=['psu'], inc=(g == 1))
                for g in range(2):
                    rs_ = slice(g * 64, (g + 1) * 64)
                    S.op('dve', lambda e, g=g, rs_=rs_: e.tensor_tensor(tmph[rs_, :].rearrange("p (h c) -> p h c", h=4), Hm[rs_, :].rearrange("p (h c) -> p h c", h=4), bc(etot[rs_, g * 4:(g + 1) * 4].unsqueeze(2), [64, 4, 64]), ALU.mult), reads=['Hm', 'etot'], writes=['tmph'])
                    S.op('dve', lambda e, g=g, rs_=rs_: e.tensor_tensor(Hm[rs_, :], tmph[rs_, :], PSi[rs_, 512 + g * 256:512 + (g + 1) * 256], ALU.add), reads=['tmph', 'psu'], writes=['Hm'])
                S.op('act', lambda e: e.activation(Hb, Hm, AF.Copy), reads=['Hm'], writes=['Hb'])
                if d_ == 0:
                    S.dma('pool', YF[b, pos:pos + 128, :], ydq, reads=['yd%d' % q])
                else:
                    z_, yf_, o_ = zt[q], yft[q], ob[q]
                    S.op('pool', lambda e, ydq=ydq, yf_=yf_: e.tensor_tensor(ydq, ydq, yf_, ALU.add), reads=['yd%d' % q, lt + 'y'], writes=['yd%d' % q])
                    S.op('dve', lambda e, x_=x_: e.tensor_tensor(tmpy.rearrange("p (h c) -> p h c", h=8), x_[:, 0:512].rearrange("p (h c) -> p h c", h=8), bc(dsk[:].unsqueeze(2), [128, 8, 64]), ALU.mult), reads=[lt + 'x'], writes=['tmpy'])
                    S.op('pool', lambda e, ydq=ydq: e.tensor_tensor(ydq, ydq, tmpy, ALU.add), reads=['tmpy', 'yd%d' % q], writes=['yd%d' % q])
                    S.op('act', lambda e, z_=z_: e.activation(z_, z_, AF.Silu), reads=[lt + 'z'], writes=[lt + 'z'])
                    S.op('dve', lambda e, ydq=ydq, z_=z_: e.tensor_tensor(ydq, ydq, z_, ALU.mult), reads=['yd%d' % q, lt + 'z'], writes=['yd%d' % q])
                    S.op('act', lambda e, ydq=ydq: e.activation(sq, ydq, AF.Square, accum_out=st[:, 0:1]), reads=['yd%d' % q], writes=['sq', 'st'])
                    rstd_from_ss(st[:, 0:1], st[:, 1:2], 512, ['st'], ['st1'])
                    S.op('dve', lambda e, ydq=ydq, o_=o_: e.scalar_tensor_tensor(o_, ydq, st[:, 1:2], ssdn[:], ALU.mult, ALU.mult), reads=['yd%d' % q, 'st1'], writes=['ob%d' % q])
                    S.dma('pool', MIX[b, pos:pos + 128, 0:512], o_, reads=['ob%d' % q])
        S.barrier()

    def p_hg(l, d_):
        import os
        A.reset()
        BM = BMf if d_ == 0 else BMb
        qk = [A.bf16(4 * 128) for _ in range(2)]
        vt = [A.bf16(256) for _ in range(2)]
        gt = [A.f32(256) for _ in range(2)]
        oft = [A.f32(256) for _ in range(2)]
        s3 = [A.f32(24) for _ in range(2)]
        qc = [A.bf16(2 * 128) for _ in range(4)]
        qh = [A.bf16(2 * 128) for _ in range(2)]
        ktc = [A.bf16(256) for _ in range(4)]
        Ssh = [A.bf16(64) for _ in range(4)]
        ktok = A.bf16(256)
        PT = A.bf16(512)
        Sm = A.f32(128)
        Ss = A.bf16(128)
        tmpu = A.f32(256)
        osb = [A.f32(256) for _ in range(2)]
        sq = A.f32(256)
        ssum = A.f32(8)
        o2 = A.f32(256)
        ob = [A.bf16(256) for _ in range(2)]
        for i in range(4):
            S.op('pool', lambda e, i=i: e.memset(qc[i], 0.0), writes=['qc%d' % i])
        for i in range(2):
            S.op('pool', lambda e, i=i: e.memset(qh[i], 0.0), writes=['qh%d' % i])
        for i in range(4):
            S.op('pool', lambda e, i=i: e.memset(Ssh[i], 0.0), writes=['Ss'])
        Smv = Sm.rearrange("p (a v) -> p a v", a=2)
        Ssv = Ss.rearrange("p (a v) -> p a v", a=2)
        ci = 0
        for b in range(NB):
            S.op('dve', lambda e: e.memset(Sm, 0.0), reads=['Ss'], writes=['Sm'])
            order = list(range(34)) if d_ == 0 else [1, 0] + list(range(33, 1, -1))
            for j in order:
                pos = j * 128
                q = ci % 2
                ci += 1
                lt = 'ld%d' % q
                qkv = qk[q].rearrange("p (a r t) -> p a r t", a=2, r=2)
                S.dma('sp', qk[q].rearrange("p (m t) -> p m t", t=128), HGQK[b, d_, :, :, :, pos:pos + 128].rearrange("a r p t -> p (a r) t"), writes=[lt + 'q'])
                S.dma('sp', vt[q], TB[b, pos:pos + 128, 256:512], writes=[lt + 'v'])
                s3v = s3[q].rearrange("p (r c t) -> p r c t", r=2, c=4)
                S.dma('sp', s3v, HGS[b, d_, :, :, pos // 32:pos // 32 + 4, :].rearrange("r p c t -> p r c t"), writes=[lt + 's'])
                if d_ == 1:
                    S.dma('sp', gt[q], TOK[b, pos:pos + 128, 512:768], writes=[lt + 'g'])
                    S.dma('sp', oft[q], OF[b, pos:pos + 128, :], writes=[lt + 'o'])
                pkt = PS[0].bitcast(BF16)[:, 1024:1280]
                for r in range(2):
                    S.op('pe', lambda e, r=r, qkv=qkv: e.transpose(pkt[:, r * 128:(r + 1) * 128], qkv[:, 1, r, :], ident[:]), reads=[lt + 'q', 'c'], writes=['pkt'], inc=(r == 1))
                S.op('act', lambda e: e.activation(ktok, pkt, AF.Copy), reads=['pkt'], writes=['ktok'])
                for c_ in range(4):
                    S.op('dve' if c_ % 2 == 0 else 'pool', lambda e, c_=c_: e.tensor_scalar(ktc[c_], ktok, RM[:, c_:c_ + 1], None, ALU.mult), reads=['ktok', 'c'], writes=['ktc%d' % c_])
                for hh in range(2 if os.environ.get('K_HGX', '') != 'noqh' else 0):
                    S.op('pool', lambda e, hh=hh, qkv=qkv: e.tensor_copy(qh[hh].rearrange("p (r t) -> p r t", r=2)[hh * 64:(hh + 1) * 64, :, :], qkv[hh * 64:(hh + 1) * 64, 0, :, :]), reads=[lt + 'q'], writes=['qh%d' % hh])
                if os.environ.get('K_HG', 'z') < 'b':
                    continue
                for h in range(4):
                    r, hr = h // 2, slice((h % 2) * 64, (h % 2) * 64 + 64)
                    S.op('pe', lambda e, h=h, r=r, qkv=qkv: e.matmul(PS[0][:, h * 128:(h + 1) * 128], qkv[:, 1, r, :], qh[h % 2].rearrange("p (r t) -> p r t", r=2)[:, r, :], start=True, stop=True), reads=[lt + 'q', 'qh%d' % (h % 2)], writes=['psS'], inc=(h == 3))
                S.op('dve', lambda e: e.tensor_tensor(PT.rearrange("p (h t) -> p h t", h=4), PS[0][:, 0:512].rearrange("p (h t) -> p h t", h=4), bc(BM[:].unsqueeze(1), [128, 4, 128]), ALU.mult), reads=['psS', 'c'], writes=['PT'])
                cords = (0, 1, 2, 3) if d_ == 0 else (3, 2, 1, 0)
                for c_ in range(4):
                    S.op('pool', lambda e, c_=c_, qkv=qkv: e.tensor_copy(qc[c_].rearrange("p (r t) -> p r t", r=2)[:, :, c_ * 32:(c_ + 1) * 32], qkv[:, 0, :, c_ * 32:(c_ + 1) * 32]), reads=[lt + 'q'], writes=['qc%d' % c_])
                POh = [PS[1][:, 512:576], PS[2][:, 0:64], PS[2][:, 512:576], PS[3][:, 0:64]]
                if os.environ.get('K_HG', 'z') < 'c':
                    continue
                for ic, c_ in enumerate(cords):
                    for h in range(4):
                        r, hr = h // 2, slice((h % 2) * 64, (h % 2) * 64 + 64)
                        S.op('dve', lambda e, h=h, r=r, hr=hr, c_=c_, s3v=s3v: e.tensor_scalar(Ssh[h][hr, :], Smv[hr, r, :], s3v[hr, r, c_, 0:1], None, ALU.mult), reads=['Sm', lt + 's'], writes=['Ss'])
                    for h in range(4):
                        r, hr = h // 2, slice((h % 2) * 64, (h % 2) * 64 + 64)
                        ov = POh[h]
                        if ic == 0:
                            S.op('pe', lambda e, h=h, ov=ov: e.matmul(ov, PT[:, h * 128:(h + 1) * 128], vt[q][:, h * 64:(h + 1) * 64], start=True, stop=False), reads=['PT', lt + 'v'], writes=['pso%d' % h], inc=False)
                        S.op('pe', lambda e, h=h, r=r, ov=ov, c_=c_, ic=ic: e.matmul(ov, qc[c_].rearrange("p (r t) -> p r t", r=2)[:, r, :], Ssh[h], start=False, stop=(ic == 3)), reads=['qc%d' % c_, 'Ss'], writes=['pso%d' % h], inc=(h == 3))
                    for h in range(4):
                        r = h // 2
                        S.op('pe', lambda e, h=h, r=r, c_=c_: e.matmul(PS[1][:, h * 64:(h + 1) * 64], ktc[c_][:, r * 128:(r + 1) * 128], vt[q][:, h * 64:(h + 1) * 64], start=True, stop=True), reads=['ktc%d' % c_, lt + 'v'], writes=['psU'], inc=(h == 3))
                    for h in range(4):
                        r, hr = h // 2, slice((h % 2) * 64, (h % 2) * 64 + 64)
                        S.op('dve', lambda e, h=h, r=r, hr=hr, c_=c_, s3v=s3v: e.tensor_scalar(tmpu[hr, h * 64:(h + 1) * 64], PS[1][hr, h * 64:(h + 1) * 64], s3v[hr, r, c_, 2:3], None, ALU.mult), reads=['psU', lt + 's'], writes=['tmpu'])
                        S.op('dve', lambda e, h=h, r=r, hr=hr, c_=c_, s3v=s3v: e.scalar_tensor_tensor(Smv[hr, r, :], Smv[hr, r, :], s3v[hr, r, c_, 1:2], tmpu[hr, h * 64:(h + 1) * 64], ALU.mult, ALU.add), reads=['tmpu', 'Sm', lt + 's'], writes=['Sm'])
                o_ = osb[q]
                if d_ == 0:
                    for h in range(4):
                        S.op('act', lambda e, o_=o_, h=h: e.activation(o_[:, h * 64:(h + 1) * 64], POh[h], AF.Copy), reads=['pso%d' % h], writes=['osb%d' % q])
                    S.dma('pool', OF[b, pos:pos + 128, :], o_, reads=['osb%d' % q])
                else:
                    for h in range(4):
                        S.op('dve', lambda e, o_=o_, h=h: e.tensor_tensor(o_[:, h * 64:(h + 1) * 64], POh[h], oft[q][:, h * 64:(h + 1) * 64], ALU.add), reads=['pso%d' % h, lt + 'o'], writes=['osb%d' % q])
                    S.op('act', lambda e, o_=o_: e.activation(sq, o_, AF.Square), reads=['osb%d' % q], writes=['sq'])
                    S.op('dve', lambda e: e.tensor_reduce(ssum[:, 0:4], sq.rearrange("p (h v) -> p h v", h=4), AX.X, ALU.add), reads=['sq'], writes=['ssum'])
                    rstd_from_ss(ssum[:, 0:4], ssum[:, 4:8], 64, ['ssum'], ['ssum1'])
                    S.op('dve', lambda e, o_=o_: e.tensor_tensor(o2.rearrange("p (h v) -> p h v", h=4), o_.rearrange("p (h v) -> p h v", h=4), bc(ssum[:, 4:8].unsqueeze(2), [128, 4, 64]), ALU.mult), reads=['osb%d' % q, 'ssum1'], writes=['o2'])
                    S.op('pool', lambda e: e.tensor_tensor(o2.rearrange("p (h v) -> p h v", h=4), o2.rearrange("p (h v) -> p h v", h=4), bc(hgn[:].unsqueeze(1), [128, 4, 64]), ALU.mult), reads=['o2'], writes=['o2'])
                    S.op('act', lambda e: e.activation(gt[q], gt[q], AF.Silu), reads=[lt + 'g'], writes=[lt + 'g'])
                    S.op('dve', lambda e: e.tensor_tensor(ob[q], o2, gt[q], ALU.mult), reads=['o2', lt + 'g'], writes=['ob%d' % q])
                    S.dma('pool', MIX[b, pos:pos + 128, 768:1024], ob[q], reads=['ob%d' % q])
        S.barrier()

    def p_na(l):
        A.reset()
        KT = A.bf16(2 * T).rearrange("p (r t) -> p r t", r=2)
        QT = A.bf16(2 * T).rearrange("p (r t) -> p r t", r=2)
        V = A.bf16(34 * 256).rearrange("p (j c) -> p j c", j=34)
        bd = [A.bf16(128) for _ in range(2)]
        sS = A.f32(768)
        Pe = A.bf16(768)
        Po = A.bf16(896)
        PTs = [A.bf16(7 * 128) for _ in range(2)]
        mx = A.f32(4)
        onb = [A.bf16(128) for _ in range(2)]
        for i in range(2):
            S.op('pool', lambda e, i=i: e.memset(bd[i], 0.0), writes=['bd%d' % i])
        S.op('pool', lambda e: e.memset(Po, 0.0), writes=['Po'])
        ui = 0
        for b in range(NB):
            S.dma('sp', QT, NAQK[b, 0:2].rearrange("r p t -> p r t"), writes=['QT'])
            S.dma('sp', KT, NAQK[b, 2:4].rearrange("r p t -> p r t"), writes=['KT'])
            for j0 in range(0, 34, 8):
                j1 = min(34, j0 + 8)
                S.dma('sp', V[:, j0:j1, :], TB[b, j0 * 128:j1 * 128, 0:256].rearrange("(j p) c -> p j c", p=128), writes=['V'])
            units = [('c', i) for i in range(4)] + [('l', i) for i in range(64)]
            for (kind, i) in units:
                for pr in range(2):
                    u = ui % 2
                    ui += 1
                    qpos = i * 64 if kind == 'c' else 256 + i * 64
                    bdt = bd[u]
                    for hh in range(2):
                        rs_ = slice(hh * 64, hh * 64 + 64)
                        S.op('pool', lambda e, bdt=bdt, rs_=rs_, pr=pr, qpos=qpos: e.tensor_copy(bdt[rs_, rs_], QT[rs_, pr, qpos:qpos + 64]), reads=['QT'], writes=['bd%d' % u])
                    if kind == 'l':
                        r0 = min(max(i - 4, 0), 56)
                        off = r0 - i + 7
                        kpos = 256 + r0 * 64
                        S.op('pe', lambda e, bdt=bdt, pr=pr, kpos=kpos: e.matmul(PS[0][:, 0:512], bdt, KT[:, pr, kpos:kpos + 512], start=True, stop=True), reads=['bd%d' % u, 'KT'], writes=['psA'], inc=False)
                    S.op('pe', lambda e, bdt=bdt, pr=pr: e.matmul(PS[0][:, 512:768], bdt, KT[:, pr, 0:256], start=True, stop=True), reads=['bd%d' % u, 'KT'], writes=['psA'])
                    if kind == 'l':
                        S.op('dve', lambda e, pr=pr, off=off: e.tensor_tensor(sS[:, 0:512], PS[0][:, 0:512], TBL[:, pr, off * 64:off * 64 + 512], ALU.add), reads=['psA', 'TBL'], writes=['sS'])
                        S.op('act', lambda e: e.activation(sS[:, 512:768], PS[0][:, 512:768], AF.Copy), reads=['psA'], writes=['sS'])
                        sv = sS[:, 0:768]
                        odd = (r0 % 2 == 1)
                        if odd:
                            pdst = [Po[:, 64:576], Po[:, 640:896]]
                            ptiles = [Po[:, k * 128:(k + 1) * 128] for k in range(7)]
                            vt0 = (256 + (r0 - 1) * 64) // 128
                            vtl = [vt0 + k for k in range(5)] + [0, 1]
                            ptag = 'Po'
                        else:
                            pdst = [Pe[:, 0:512], Pe[:, 512:768]]
                            ptiles = [Pe[:, k * 128:(k + 1) * 128] for k in range(6)]
                            vt0 = (256 + r0 * 64) // 128
                            vtl = [vt0 + k for k in range(4)] + [0, 1]
                            ptag = 'Pe'
                    else:
                        S.op('act', lambda e: e.activation(sS[:, 512:768], PS[0][:, 512:768], AF.Copy), reads=['psA'], writes=['sS'])
                        sv = sS[:, 512:768]
                        pdst = [Pe[:, 512:768]]
                        ptiles = [Pe[:, 512:640], Pe[:, 640:768]]
                        vtl = [0, 1]
                        ptag = 'Pe'
                    S.op('dve', lambda e, sv=sv: e.tensor_reduce(mx[:, 0:1], sv, AX.X, ALU.max), reads=['sS'], writes=['mx'])
                    S.op('dve', lambda e: e.tensor_scalar(mx[:, 1:2], mx[:, 0:1], -1.0, None, ALU.mult), reads=['mx'], writes=['mx1'])
                    if kind == 'l':
                        S.op('act', lambda e, pdst=pdst: e.activation(pdst[0], sS[:, 0:512], AF.Exp, bias=mx[:, 1:2], accum_out=mx[:, 2:3]), reads=['sS', 'mx1'], writes=[ptag, 'mx2'])
                        S.op('act', lambda e, pdst=pdst: e.activation(pdst[1], sS[:, 512:768], AF.Exp, bias=mx[:, 1:2], accum_out=mx[:, 3:4]), reads=['sS', 'mx1'], writes=[ptag, 'mx3'])
                        S.op('dve', lambda e: e.tensor_tensor(mx[:, 2:3], mx[:, 2:3], mx[:, 3:4], ALU.add), reads=['mx2', 'mx3'], writes=['mx2'])
                    else:
                        S.op('act', lambda e, pdst=pdst: e.activation(pdst[0], sS[:, 512:768], AF.Exp, bias=mx[:, 1:2], accum_out=mx[:, 2:3]), reads=['sS', 'mx1'], writes=[ptag, 'mx2'])
                    S.op('dve', lambda e: e.reciprocal(mx[:, 2:3], mx[:, 2:3]), reads=['mx2'], writes=['mx2'])
                    nt_ = len(ptiles)
                    ptp = PS[1 + (u % 2)].bitcast(BF16)[:, 0:nt_ * 128]
                    ptt = 'pst%d' % (u % 2)
                    for k in range(nt_):
                        S.op('pe', lambda e, k=k, ptp=ptp, ptiles=ptiles: e.transpose(ptp[:, k * 128:(k + 1) * 128], ptiles[k], ident[:]), reads=[ptag, 'c'], writes=[ptt], inc=(k == nt_ - 1))
                    pts = PTs[u]
                    S.op('dve', lambda e, pts=pts, ptp=ptp, nt_=nt_: e.tensor_copy(pts[:, 0:nt_ * 128], ptp), reads=[ptt], writes=['PTs%d' % u])
                    for k in range(nt_):
                        S.op('pe', lambda e, k=k, pts=pts, vtl=vtl, pr=pr, nt_=nt_: e.matmul(PS[3][:, (u % 2) * 128:(u % 2) * 128 + 128], pts[:, k * 128:(k + 1) * 128], V[:, vtl[k], pr * 128:(pr + 1) * 128], start=(k == 0), stop=(k == nt_ - 1)), reads=['PTs%d' % u, 'V'], writes=['psO%d' % (u % 2)], inc=(k == nt_ - 1))
                    on = onb[u]
                    S.op('dve', lambda e, on=on: e.tensor_scalar(on, PS[3][:, (u % 2) * 128:(u % 2) * 128 + 128], mx[:, 2:3], None, ALU.mult), reads=['psO%d' % (u % 2), 'mx2'], writes=['on%d' % u])
                    for hh in range(2):
                        rs_ = slice(hh * 64, hh * 64 + 64)
                        hcol = 512 + (pr * 2 + hh) * 64
                        S.dma('pool', MIX[b, qpos:qpos + 64, hcol:hcol + 64], on[rs_, rs_], reads=['on%d' % u])
        S.barrier()

    def epilogue(b, j, psq, ptags, bufs, q, last):
        hres, tmp, sq, st = bufs
        if hres[0] is hres[1]:
            q = 0
        mj = 2 if j < 2 else b
        h = hres[q]
        S.dma('sp', h, Hs[b, j * 128:(j + 1) * 128, :], writes=['hr%d' % q])
        for hf in range(2):
            S.op('act', lambda e, hf=hf: e.activation(sq[:, hf * 512:(hf + 1) * 512], psq[:, hf * 512:(hf + 1) * 512], AF.Square, accum_out=st[:, hf:hf + 1]), reads=ptags, writes=['tmp', 'st%d' % hf])
        S.op('dve', lambda e: e.tensor_tensor(st[:, 2:3], st[:, 0:1], st[:, 1:2], ALU.add), reads=['st0', 'st1'], writes=['st2'])
        rstd_from_ss(st[:, 2:3], st[:, 3:4], D, ['st2'], ['st3'])
        for hf in range(2):
            cs = slice(hf * 512, (hf + 1) * 512)
            S.op('dve', lambda e, cs=cs, mj=mj: e.scalar_tensor_tensor(tmp[:, cs], psq[:, cs], st[:, 3:4], MSg[:, mj, cs], ALU.mult, ALU.mult), reads=ptags + ['st3'], writes=['tmp'])
        S.op('pool', lambda e, h=h: e.tensor_tensor(h, h, tmp, ALU.add), reads=['tmp', 'hr%d' % q], writes=['hr%d' % q])
        if last and j >= 2:
            S.dma('pool', y[b, (j - 2) * 128:(j - 1) * 128, :], h, reads=['hr%d' % q])
        else:
            S.dma('pool', Hs[b, j * 128:(j + 1) * 128, :], h, reads=['hr%d' % q])

    def p_outproj(l):
        A.reset()
        wo = A.bf16(8 * D).rearrange("p (k c) -> p k c", k=8)
        stg = [A.f32(4096) for _ in range(2)]
        load_cast(wo, w_out[l].rearrange("(k p) c -> p k c", p=128), 8, D, stg, 'wo')
        S.barrier()
        A.off -= 2 * 4096
        mx_ = [A.bf16(D) for _ in range(2)]
        mT = [A.bf16(D) for _ in range(2)]
        bufs = ([A.f32(D) for _ in range(2)], A.f32(D), A.f32(D), A.f32(4))
        ci = 0
        for b in range(NB):
            for j in range(34):
                q = ci % 2
                ci += 1
                S.dma('sp', mx_[q], MIX[b, j * 128:(j + 1) * 128, :], writes=['mx%d' % q])
                pv = PS[q].bitcast(BF16)[:, 0:1024]
                for k in range(8):
                    S.op('pe', lambda e, k=k, pv=pv, q=q: e.transpose(pv[:, k * 128:(k + 1) * 128], mx_[q][:, k * 128:(k + 1) * 128], ident[:]), reads=['mx%d' % q, 'c'], writes=['pT%d' % q], inc=(k == 7))
                S.op('act', lambda e, pv=pv, q=q: e.activation(mT[q], pv, AF.Copy), reads=['pT%d' % q], writes=['mT%d' % q])
                pa = PS[2 + q]
                for hf in range(2):
                    for k in range(8):
                        S.op('pe', lambda e, k=k, hf=hf, pa=pa, q=q: e.matmul(pa[:, hf * 512:(hf + 1) * 512], mT[q][:, k * 128:(k + 1) * 128], wo[:, k, hf * 512:(hf + 1) * 512], start=(k == 0), stop=(k == 7)), reads=['mT%d' % q, 'wo'], writes=['pa%d' % q], inc=(k == 7 and hf == 1))
                epilogue(b, j, pa, ['pa%d' % q], bufs, q, False)
        S.barrier()

    def p_ffn(l, last):
        A.reset()
        wu = A.bf16(8 * 2 * DFF).rearrange("p (k c) -> p k c", k=8)
        wd = A.bf16(22 * D).rearrange("p (k c) -> p k c", k=22)
        stg = [A.f32(2816) for _ in range(2)]
        load_cast(wu, w_up[l].rearrange("(k p) c -> p k c", p=128), 8, 2 * DFF, stg, 'wu', smax=2816)
        load_cast(wd, w_down[l].rearrange("(k p) c -> p k c", p=128), 22, D, stg, 'wd', smax=2816)
        S.barrier()
        A.off -= 2 * 2816
        if dbg:
            S.dma('pool', DBGW, wd.rearrange("p k c -> p (k c)"), reads=['wd'])
            for k in range(8):
                S.dma('pool', DBGU[:, k * 2 * DFF:(k + 1) * 2 * DFF], wu[:, k, :], reads=['wu'])
        vt1 = A.bf16(8 * 514)
        vts = [vt1, vt1]
        gT = A.bf16(22 * 512).rearrange("p (k c) -> p k c", k=22)
        ctmp = A.f32(512)
        sg = A.f32(512)
        hr1 = A.f32(D)
        tmp1 = A.f32(D)
        bufs = ([hr1, hr1], tmp1, tmp1, A.f32(4))
        si = 0
        ei = 0
        for b in range(NB):
            sts = [(0, 256)] + [(256 + 512 * m, 512) for m in range(8)]
            if last:
                sts = sts[1:]
            for (p0, n) in sts:
                vt_ = vts[si % 2].rearrange("p (k c) -> p k c", k=8)
                vtag = 'vt0'
                si += 1
                c0 = ucol(p0)
                S.dma('sp', vt_[:, :, 0:n + 2], UT[b].rearrange("(k p) c -> p k c", p=128)[:, :, c0 - 1:c0 + n + 1], writes=[vtag])
                npc = n // 256
                for f in range(22):
                    pg = PS[f % 2]
                    pu = PS[2][:, (f % 2) * 512:(f % 2) * 512 + 512]
                    gtag, utag_ = 'pg%d' % (f % 2), 'pu%d' % (f % 2)
                    for pc in range(npc):
                        for k in range(8):
                            S.op('pe', lambda e, pg=pg, pc=pc, k=k, f=f, vt_=vt_: e.matmul(pg[:, pc * 512:pc * 512 + 258], wu[:, k, f * 128:(f + 1) * 128], vt_[:, k, pc * 256:pc * 256 + 258], start=(k == 0), stop=(k == 7)), reads=['wu', vtag], writes=[gtag], inc=(k == 7 and pc == npc - 1))
                    for k in range(8):
                        S.op('pe', lambda e, pu=pu, k=k, f=f, vt_=vt_, n=n: e.matmul(pu[:, 0:n], wu[:, k, DFF + f * 128:DFF + (f + 1) * 128], vt_[:, k, 1:1 + n], start=(k == 0), stop=(k == 7)), reads=['wu', vtag], writes=[utag_], inc=(k == 7))
                    pv = pg.rearrange("p (a c) -> p a c", c=512)[:, 0:npc, :]
                    cv = ctmp[:, 0:n].rearrange("p (a c) -> p a c", c=256)
                    S.op('act', lambda e, pv=pv, cv=cv, f=f: e.activation(cv, pv[:, :, 0:256], AF.Copy, scale=fcw[:, f, 0:1]), reads=[gtag, 'fcw'], writes=['ctmp'])
                    S.op('dve', lambda e, pv=pv, cv=cv, f=f: e.scalar_tensor_tensor(cv, pv[:, :, 1:257], fcw[:, f, 1:2], cv, ALU.mult, ALU.add), reads=[gtag, 'ctmp'], writes=['ctmp'])
                    S.op('dve', lambda e, pv=pv, cv=cv, f=f: e.scalar_tensor_tensor(cv, pv[:, :, 2:258], fcw[:, f, 2:3], cv, ALU.mult, ALU.add), reads=[gtag, 'ctmp'], writes=['ctmp'])
                    S.op('act', lambda e, f=f, n=n: e.activation(sg[:, 0:n], ctmp[:, 0:n], AF.Silu, bias=fcw[:, f, 3:4]), reads=['ctmp'], writes=['sg'])
                    S.op('dve', lambda e, f=f, n=n, pu=pu: e.tensor_tensor(gT[:, f, 0:n], sg[:, 0:n], pu[:, 0:n], ALU.mult), reads=['sg', utag_], writes=['gT'])
                for s_ in range(n // 128):
                    q = ei % 2
                    ei += 1
                    j = (p0 + s_ * 128) // 128
                    pa = PS[3] if q == 0 else PS[2]
                    ptg = ['pa%d' % q] if q == 0 else ['pu0', 'pu1']
                    for hf in range(2):
                        for k in range(22):
                            S.op('pe', lambda e, k=k, hf=hf, pa=pa, s_=s_: e.matmul(pa[:, hf * 512:(hf + 1) * 512], gT[:, k, s_ * 128:(s_ + 1) * 128], wd[:, k, hf * 512:(hf + 1) * 512], start=(k == 0), stop=(k == 21)), reads=['gT', 'wd'], writes=ptg, inc=(k == 21 and hf == 1))
                    epilogue(b, j, pa, ptg, bufs, q, last)
        S.barrier()

    p_init()
    for l in range(NL):
        last = (l == 3)
        p_small(l)
        p_ada(l, 0)
        if stop_after == 'ada':
            break
        p_normT(l)
        if stop_after == 'normT':
            break
        p_inproj(l)
        if stop_after == 'inproj':
            break
        p_ssd(l, 0)
        p_ssd(l, 1)
        if stop_after == 'ssd':
            break
        p_hg(l, 0)
        p_hg(l, 1)
        if stop_after == 'hg':
            break
        p_na(l)
        if stop_after == 'na':
            break
        p_outproj(l)
        if stop_after == 'outproj':
            break
        p_ada(l, 1)
        p_normT(l)
        if stop_after == 'normT2':
            break
        p_ffn(l, last)
    S.finish()
    return nc, S


def host_prep(inputs):
    f = lambda a: np.ascontiguousarray(np.asarray(a, dtype=np.float32))
    rpb = f(inputs['na_rpb'])
    j = np.arange(64)[:, None]
    c = np.arange(64)[None, :]
    idx = np.clip(c - j + 15, 0, 30)
    g = rpb[:, :, :, idx]
    g = np.transpose(g, (0, 1, 3, 2, 4)).reshape(4, 2, 128, 15 * 64)
    start = np.clip(np.arange(64) - 8, 0, 48)[:, None]
    inwin = (c >= start) & (c < start + 16)
    mask = np.where(inwin, 0.0, -30000.0).astype(np.float32)
    mask = np.concatenate([mask, mask], axis=0)
    shared = {
        'c_ctx': f(inputs['c_ctx']).reshape(1, D),
        'ada_w': f(inputs['ada_w']),
        'ada_b': f(inputs['ada_b']).reshape(4, 1, 6 * D),
        'norms': np.ascontiguousarray(np.stack([f(inputs['norm_mix_pre']), f(inputs['norm_mix_post']), f(inputs['norm_ffn_pre']), f(inputs['norm_ffn_post'])], axis=1)),
        'w_in': f(inputs['w_in']),
        'ssd_cw': np.ascontiguousarray(np.concatenate([f(inputs['ssd_conv_w']), f(inputs['ssd_conv_b'])[:, None, :]], axis=1)),
        'ssd_dtb': f(inputs['ssd_dt_bias']).reshape(4, 1, 16),
        'ssd_alog': f(inputs['ssd_a_log']).reshape(4, 1, 16),
        'ssd_d': f(inputs['ssd_d']).reshape(4, 1, 8),
        'ssd_norm': f(inputs['ssd_norm']).reshape(4, 1, 512),
        'rpbT': np.ascontiguousarray(g),
        'namask': mask,
        'hg_lbl': f(inputs['hg_lb_logits']),
        'hg_norm': f(inputs['hg_norm']).reshape(4, 1, 64),
        'w_out': f(inputs['w_out']),
        'w_up': f(inputs['ffn_w_up']),
        'ffn_cw': np.ascontiguousarray(np.concatenate([f(inputs['ffn_conv_w']), f(inputs['ffn_conv_b'])[:, None, :]], axis=1)),
        'w_down': f(inputs['ffn_w_down']),
    }
    return shared


def kernel(**inputs):
    shared = host_prep(inputs)
    x = np.asarray(inputs['x'], dtype=np.float32)
    c = np.asarray(inputs['c'], dtype=np.float32)
    ctx = np.asarray(inputs['ctx'], dtype=np.float32)
    nc, _ = build()
    in_maps = []
    for i in range(8):
        m = dict(shared)
        m['x'] = np.ascontiguousarray(x[2 * i:2 * i + 2])
        m['c'] = np.ascontiguousarray(c[2 * i:2 * i + 2])
        m['ctx'] = np.ascontiguousarray(ctx[2 * i:2 * i + 2])
        in_maps.append(m)
    res = run_bass_kernel_spmd(nc, in_maps, core_ids=list(range(8)))
    return np.concatenate([np.asarray(r['y'], dtype=np.float32) for r in res.results], axis=0)
```

```python
import numpy as np
import concourse.bass as bass
import concourse.mybir as mybir
from concourse.bass_utils import run_bass_kernel_spmd

F32 = mybir.dt.float32
BF16 = mybir.dt.bfloat16
AF = mybir.ActivationFunctionType
ALU = mybir.AluOpType
AX = mybir.AxisListType

D = 1024
T = 4352
UTW = T + 3
DFF = 2816
EPS = 1e-6


def ucol(p):
    return p + 1 if p < 256 else p + 2


class _Rec:
    def __init__(self):
        self.call = None

    def __getattr__(self, name):
        def f(*a, **k):
            self.call = (name, a, k)
            return self
        return f


class Sched:
    ENG = ('pe', 'act', 'dve', 'pool', 'sp')

    def __init__(self, nc, nd_sp=14, nd_pool=10):
        self.nc = nc
        self.thunks = {k: [] for k in self.ENG}
        self.cnt = {k: 0 for k in self.ENG}
        self.NSET = 16
        self.semobj = {}
        self.semval = {}
        for i in range(self.NSET):
            for k in ('pe', 'act', 'dve', 'pool'):
                nm = 's_%s_%d' % (k, i)
                self.semobj[nm] = nc.alloc_semaphore(nm)
                self.semval[nm] = 0
        self.set = 0
        self.own = {k: 's_%s_0' % k for k in ('pe', 'act', 'dve', 'pool')}
        self.sem = {k: self.semobj[self.own[k]] for k in ('pe', 'act', 'dve', 'pool')}
        self.waited = {k: {} for k in self.ENG}
        self.regs = {}
        self.dq = {}
        for q, n in (('sp', nd_sp), ('pool', nd_pool)):
            names = ['d%s%d' % (q, i) for i in range(n)]
            for nm in names:
                self.semobj[nm] = nc.alloc_semaphore(nm)
            self.dq[q] = [names, 0]
        self.dlast = {}
        self.ninstr = 0

    def _deps(self, reads, writes):
        ev = {}

        def add(e):
            if e is not None and ev.get(e[0], 0) < e[1]:
                ev[e[0]] = e[1]
        for r in reads:
            st = self.regs.get(r)
            if st:
                add(st[0])
        for w in writes:
            st = self.regs.get(w)
            if st:
                add(st[0])
                for s, v in st[1].items():
                    add((s, v))
        return ev

    def _commit(self, reads, writes, e):
        for r in reads:
            st = self.regs.setdefault(r, [None, {}])
            if st[1].get(e[0], 0) < e[1]:
                st[1][e[0]] = e[1]
        for w in writes:
            self.regs[w] = [e, {}]

    def _emit_waits(self, X, ev, skip=None):
        for s, v in ev.items():
            if s == skip or self.waited[X].get(s, 0) >= v:
                continue
            self.waited[X][s] = v
            so = self.semobj[s]
            self.thunks[X].append(lambda e, so=so, v=v: e.wait_ge(so, v))
            self.ninstr += 1

    def op(self, X, fn, reads=(), writes=(), inc=True):
        rec = _Rec()
        fn(rec)
        name_, a_, k_ = rec.call
        fn = lambda eng, name_=name_, a_=a_, k_=k_: getattr(eng, name_)(*a_, **k_)
        own = self.own[X]
        ev = self._deps(reads, writes)
        if X != 'pe' and own in ev:
            evr = self._deps(reads, ())
            if own in evr:
                ev[own] = evr[own]
            else:
                del ev[own]
        self._emit_waits(X, ev, skip=own if X == 'pe' else None)
        e = (own, self.cnt[X] + 1)
        if inc:
            self.cnt[X] += 1
            so = self.sem[X]
            self.thunks[X].append(lambda eng, fn=fn, so=so: fn(eng).then_inc(so, 1))
        else:
            self.thunks[X].append(lambda eng, fn=fn: fn(eng))
        self.ninstr += 1
        self._commit(reads, writes, e)

    def dma(self, Q, out, in_, reads=(), writes=(), **kw):
        ev = self._deps(reads, writes)
        names, j = self.dq[Q]
        self.dq[Q][1] += 1
        K = len(names)
        name = names[j % K]
        if j >= K and ev.get(name, 0) < 16 * (j // K):
            ev[name] = 16 * (j // K)
        self._emit_waits(Q, ev)
        e = (name, 16 * (j // K + 1))
        self.dlast[name] = e[1]
        so = self.semobj[name]
        self.thunks[Q].append(lambda eng, so=so, out=out, in_=in_, kw=kw: eng.dma_start(out=out, in_=in_, **kw).then_inc(so, 16))
        self.ninstr += 1
        self._commit(reads, writes, e)
        return e

    def barrier(self):
        ev = dict(self.dlast)
        for k in ('pe', 'act', 'dve', 'pool'):
            if self.cnt[k] > 0:
                ev[self.own[k]] = self.cnt[k]
        for X in self.ENG:
            self._emit_waits(X, ev, skip=self.own.get(X))
        self.regs = {}
        for k in ('pe', 'act', 'dve', 'pool'):
            self.semval[self.own[k]] = self.cnt[k]
        self.set = (self.set + 1) % self.NSET
        for k in ('pe', 'act', 'dve', 'pool'):
            self.own[k] = 's_%s_%d' % (k, self.set)
            self.sem[k] = self.semobj[self.own[k]]
            self.cnt[k] = self.semval[self.own[k]]

    def finish(self):
        self.barrier()
        nc = self.nc
        th = self.thunks
        with nc.Block() as block:
            @block.tensor
            def _(e):
                for t in th['pe']:
                    t(e)

            @block.scalar
            def _(e):
                for t in th['act']:
                    t(e)

            @block.vector
            def _(e):
                for t in th['dve']:
                    t(e)

            @block.gpsimd
            def _(e):
                for t in th['pool']:
                    t(e)

            @block.sync
            def _(e):
                for t in th['sp']:
                    t(e)


class Arena:
    def __init__(self, ap, ncols):
        self.ap = ap
        self.n = ncols
        self.n0 = ncols
        self.off = 0

    def reset(self, reserve=0):
        self.off = 0
        self.n = self.n0 - reserve

    def f32(self, cols):
        cols_al = (cols + 7) // 8 * 8
        assert self.off + cols_al <= self.n, ('arena overflow', self.off, cols_al, self.n)
        v = self.ap[:, self.off:self.off + cols]
        self.off += cols_al
        return v

    def bf16(self, cols):
        c32 = (cols + 1) // 2
        c32 = (c32 + 7) // 8 * 8
        assert self.off + c32 <= self.n, ('arena overflow', self.off, c32, self.n)
        v = self.ap[:, self.off:self.off + c32].bitcast(BF16)[:, 0:cols]
        self.off += c32
        return v


def bc(ap, shape):
    return ap.to_broadcast(list(shape))


def build(NL=4, NB=2, dbg=False, stop_after=None):
    nc = bass.Bass("TRN2", target_bir_lowering=False)

    def din(name, shape, dt=F32):
        return nc.dram_tensor(name, list(shape), dt, kind="ExternalInput").ap()

    def dscr(name, shape, dt=F32):
        kind = "ExternalOutput" if dbg else "Internal"
        return nc.dram_tensor(name, list(shape), dt, kind=kind).ap()

    x = din('x', [NB, 4096, D])
    cc = din('c', [NB, D])
    ctx = din('ctx', [NB, 256, D])
    c_ctx = din('c_ctx', [1, D])
    ada_w = din('ada_w', [4, D, 6 * D])
    ada_b = din('ada_b', [4, 1, 6 * D])
    nrm = din('norms', [4, 4, D])
    w_in = din('w_in', [4, D, 3344])
    ssd_cw = din('ssd_cw', [4, 4, 768])
    ssd_dtb = din('ssd_dtb', [4, 1, 16])
    ssd_alog = din('ssd_alog', [4, 1, 16])
    ssd_d = din('ssd_d', [4, 1, 8])
    ssd_norm = din('ssd_norm', [4, 1, 512])
    rpbT = din('rpbT', [4, 2, 128, 960])
    namask = din('namask', [128, 64])
    hg_lbl = din('hg_lbl', [4, 256])
    hg_norm = din('hg_norm', [4, 1, 64])
    w_out = din('w_out', [4, D, D])
    w_up = din('w_up', [4, D, 2 * DFF])
    ffn_cw = din('ffn_cw', [4, 4, DFF])
    w_down = din('w_down', [4, DFF, D])
    y = nc.dram_tensor('y', [NB, 4096, D], F32, kind="ExternalOutput").ap()

    Hs = dscr('Hs', [NB, T, D])
    UT = dscr('UT', [NB, D, UTW], BF16)
    XB = dscr('XB', [NB, T, 640], BF16)
    BCs = dscr('BCs', [NB, 2, 128, T], BF16)
    TOK = dscr('TOK', [NB, T, 784])
    TB = dscr('TB', [NB, T, 512], BF16)
    NAQK = dscr('NAQK', [NB, 4, 128, T], BF16)
    HGQK = dscr('HGQK', [NB, 2, 2, 2, 128, T], BF16)
    HGS = dscr('HGS', [NB, 2, 2, 128, 136, 3])
    YF = dscr('YF', [NB, T, 512])
    OF = dscr('OF', [NB, T, 256])
    MIX = dscr('MIX', [NB, T, D], BF16)
    DBG = dscr('DBG', [128, 128])
    DBGW = dscr('DBGW', [128, 22 * D], BF16)
    DBGU = dscr('DBGU', [128, 8 * 2 * DFF], BF16)

    S = Sched(nc)
    sb = nc.alloc_sbuf_tensor
    ident = sb('ident', [128, 128], BF16)
    ones = sb('ones', [128, 512], F32)
    TRIf = sb('TRIf', [128, 128], F32)
    TRIb = sb('TRIb', [128, 128], F32)
    MGT = sb('MGT', [128, 128], F32)
    MLT = sb('MLT', [128, 128], F32)
    BMf = sb('BMf', [128, 128], F32)
    BMb = sb('BMb', [128, 128], F32)
    zer = sb('zer', [128, 64], BF16)
    RT = sb('RT', [4, 128], F32)
    I4 = sb('I4', [4, 4], F32)
    RM = sb('RM', [128, 4], F32)
    sT = sb('sT', [128, 3, 8], F32)
    LB = sb('LB', [128, 2, 4], F32)
    OML = sb('OML', [128, 2, 4], F32)
    NOML = sb('NOML', [128, 2, 4], F32)
    MSg = sb('MSg', [128, 3, D], F32)
    dtb = sb('dtb', [128, 16], F32)
    aneg = sb('aneg', [128, 16], F32)
    dsk = sb('dsk', [128, 8], F32)
    ssdn = sb('ssdn', [128, 512], F32)
    hgn = sb('hgn', [128, 64], F32)
    scw = sb('scw', [128, 6, 4], F32)
    fcw = sb('fcw', [128, 22, 4], F32)
    TBL = sb('TBL', [128, 2, 960], F32)
    epsc = sb('epsc', [128, 1], F32)
    AW = (nc.sbuf_bytes_remaining - 2048) // 4 // 8 * 8
    arena_t = sb('arena', [128, AW], F32)
    A = Arena(arena_t, AW)
    MSA_N = 3 * 2 * D
    MSa = arena_t[:, AW - MSA_N:AW].rearrange("p (j a c) -> p j a c", j=3, a=2)
    PS = [nc.alloc_psum_tensor('ps%d' % i, [128, 1024], F32) for i in range(4)]

    def pool_const():
        S.op('pool', lambda e: e.memset(ident[:], 1.0), writes=['c'])
        S.op('pool', lambda e: e.affine_select(ident[:], ident[:], [[-1, 128]], ALU.is_equal, 0.0, base=0, channel_multiplier=1), writes=['c'])
        S.op('pool', lambda e: e.memset(ones[:], 1.0), writes=['c'])
        S.op('pool', lambda e: e.memset(zer[:], 0.0), writes=['c'])
        S.op('pool', lambda e: e.memset(epsc[:], EPS), writes=['c'])
        for m, pat, cm, op in ((TRIf, 1, -1, ALU.is_ge), (TRIb, -1, 1, ALU.is_ge), (MGT, -1, 1, ALU.is_gt), (MLT, 1, -1, ALU.is_gt),
                               (BMf, 1, -1, ALU.is_ge), (BMb, -1, 1, ALU.is_ge)):
            S.op('pool', lambda e, m=m: e.memset(m[:], 1.0), writes=['c'])
            S.op('pool', lambda e, m=m, pat=pat, cm=cm, op=op: e.affine_select(m[:], m[:], [[pat, 128]], op, 0.0, base=0, channel_multiplier=cm), writes=['c'])
        S.op('pool', lambda e: e.memset(RT[:], 1.0), writes=['c'])
        S.op('pool', lambda e: e.affine_select(RT[:], RT[:], [[1, 128]], ALU.is_ge, 0.0, base=0, channel_multiplier=-32), writes=['c'])
        S.op('pool', lambda e: e.affine_select(RT[:], RT[:], [[-1, 128]], ALU.is_ge, 0.0, base=31, channel_multiplier=32), writes=['c'])
        S.op('pool', lambda e: e.memset(I4[:], 1.0), writes=['c'])
        S.op('pool', lambda e: e.affine_select(I4[:], I4[:], [[-1, 4]], ALU.is_equal, 0.0, base=0, channel_multiplier=1), writes=['c'])
        S.op('pe', lambda e: e.matmul(PS[0][:, 0:128], RT[:], RT[:], start=True, stop=True), reads=['c'], writes=['cps'], inc=False)
        S.op('pe', lambda e: e.matmul(PS[0][:, 128:132], RT[:], I4[:], start=True, stop=True), reads=['c'], writes=['cps'])
        S.op('dve', lambda e: e.tensor_tensor(BMf[:], BMf[:], PS[0][:, 0:128], ALU.mult), reads=['c', 'cps'], writes=['c'])
        S.op('dve', lambda e: e.tensor_tensor(BMb[:], BMb[:], PS[0][:, 0:128], ALU.mult), reads=['c', 'cps'], writes=['c'])
        S.op('dve', lambda e: e.tensor_copy(RM[:], PS[0][:, 128:132]), reads=['cps'], writes=['c'])

    pool_const()

    def row_bcast(dst, src_row_dram, n, scratch_row, psv, post=None):
        S.dma('sp', scratch_row[0:1, 0:n], src_row_dram, writes=['rb_row'])
        for o in range(0, n, 512):
            w = min(512, n - o)
            S.op('pe', lambda e, o=o, w=w: e.matmul(psv[:, 0:w], ones[0:1, 0:128], scratch_row[0:1, o:o + w], start=True, stop=True), reads=['rb_row', 'c'], writes=['rb_ps'])
            if post is None:
                S.op('dve', lambda e, o=o, w=w: e.tensor_copy(dst[:, o:o + w], psv[:, 0:w]), reads=['rb_ps'], writes=['rb_dst'])
            else:
                post(o, w)

    def load_cast(dst, src, nk, cols, stg, tag, smax=4096):
        step = max(1, min(cols, smax // nk))
        i = 0
        for o in range(0, cols, step):
            w = min(step, cols - o)
            st = stg[i % 2]
            for k0 in range(0, nk, 8):
                k1 = min(nk, k0 + 8)
                S.dma('sp', st[:, 0:nk * w].rearrange("p (k c) -> p k c", c=w)[:, k0:k1, :], src[:, k0:k1, o:o + w], writes=['stg%d' % (i % 2)])
            eng = 'pool' if i % 2 == 0 else 'act'
            if eng == 'pool':
                S.op('pool', lambda e, st=st, o=o, w=w: e.tensor_copy(dst[:, :, o:o + w], st[:, 0:nk * w].rearrange("p (k c) -> p k c", c=w)), reads=['stg%d' % (i % 2)], writes=[tag])
            else:
                S.op('act', lambda e, st=st, o=o, w=w: e.activation(dst[:, :, o:o + w], st[:, 0:nk * w].rearrange("p (k c) -> p k c", c=w), AF.Copy), reads=['stg%d' % (i % 2)], writes=[tag])
            i += 1

    def rstd_from_ss(ss, rs, n, tagr, tagw):
        S.op('act', lambda e: e.activation(rs, ss, AF.Sqrt, bias=epsc[:, 0:1], scale=1.0 / n), reads=tagr + ['c'], writes=tagw)
        S.op('dve', lambda e: e.reciprocal(rs, rs), reads=tagw, writes=tagw)

    def p_init():
        A.reset()
        for b in range(NB):
            S.dma('sp', Hs[b, 0:256, :], ctx[b])
            for r in range(0, 4096, 1024):
                S.dma('sp', Hs[b, 256 + r:256 + r + 1024, :], x[b, r:r + 1024, :])
            for col in (0, 257, UTW - 1):
                S.dma('sp', UT[b].rearrange("(k p) c -> p k c", p=128)[:, :, col:col + 1], zer[:, 0:8].rearrange("p (k o) -> p k o", o=1), reads=['c'], allow_slow_non_contiguous=True)
        for j in range(3):
            src = cc[j:j + 1, :] if j < 2 and j < NB else (c_ctx if j == 2 else cc[0:1, :])
            S.dma('sp', sT[:, j, :], src.rearrange("o (k p) -> p (o k)", p=128), writes=['sT'], allow_slow_non_contiguous=True)
        S.op('act', lambda e: e.activation(sT[:], sT[:], AF.Silu), reads=['sT'], writes=['sT'])
        ex = A.f32(8)
        tot = A.f32(2)
        for c_ in range(2):
            for l_ in range(4):
                S.dma('sp', ex[:, c_ * 4 + l_:c_ * 4 + l_ + 1], hg_lbl[l_:l_ + 1, c_ * 128:(c_ + 1) * 128].rearrange("o p -> p o"), writes=['ex'], allow_slow_non_contiguous=True)
        S.op('act', lambda e: e.activation(ex, ex, AF.Exp), reads=['ex'], writes=['ex'])
        exv = ex.rearrange("p (c l) -> p c l", l=4)
        S.op('dve', lambda e: e.tensor_reduce(tot, exv, AX.X, ALU.add), reads=['ex'], writes=['tot'])
        S.op('dve', lambda e: e.reciprocal(tot, tot), reads=['tot'], writes=['tot'])
        S.op('dve', lambda e: e.memset(LB[:], 0.0), writes=['LB'])
        for l in range(1, 4):
            S.op('dve', lambda e, l=l: e.tensor_tensor(LB[:, :, l], LB[:, :, l - 1], exv[:, :, l], ALU.add), reads=['LB', 'ex'], writes=['LB'])
        S.op('dve', lambda e: e.tensor_tensor(LB[:], LB[:], bc(tot.unsqueeze(2), [128, 2, 4]), ALU.mult), reads=['LB', 'tot'], writes=['LB'])
        S.op('dve', lambda e: e.tensor_scalar(OML[:], LB[:], -1.0, 1.0, ALU.mult, ALU.add), reads=['LB'], writes=['OML'])
        S.op('dve', lambda e: e.tensor_scalar(NOML[:], OML[:], -1.0, None, ALU.mult), reads=['OML'], writes=['NOML'])
        S.barrier()

    def p_ada(l, half):
        A.reset(MSA_N)
        srep = A.f32(3 * 8 * 128).rearrange("p (j k c) -> p j k c", j=3, k=8)
        awb = [A.f32(8 * 512) for _ in range(2)]
        abr = [A.f32(512) for _ in range(2)]
        nrow = A.f32(2 * D).rearrange("p (a c) -> p a c", a=2)
        rowt = A.f32(D)
        for j in range(3):
            S.op('dve', lambda e, j=j: e.tensor_copy(srep[:, j], bc(sT[:, j, :].unsqueeze(2), [128, 8, 128])), reads=['sT'], writes=['srep'])
        for a in range(2):
            row_bcast(nrow[:, a, :], nrm[l, 2 * half + a:2 * half + a + 1, :], D, rowt, PS[3][:, 0:512])
        for cgi in range(6):
            cg = half * 6 + cgi
            kind, hh = cgi // 2, cgi % 2
            aw = awb[cgi % 2]
            ab = abr[cgi % 2]
            awv = aw.rearrange("p (k c) -> p k c", c=512)
            S.dma('sp', awv, ada_w[l].rearrange("(k p) c -> p k c", p=128)[:, :, cg * 512:(cg + 1) * 512], writes=['aw%d' % (cgi % 2)])
            S.dma('sp', ab[0:1, :], ada_b[l, :, cg * 512:(cg + 1) * 512], writes=['ab%d' % (cgi % 2)])
            for j in range(3):
                pv = PS[j % 2][:, 0:512]
                pt = 'adaps%d' % (j % 2)
                for k in range(8):
                    S.op('pe', lambda e, j=j, k=k, pv=pv, awv=awv: e.matmul(pv, srep[:, j, k, :], awv[:, k, :], start=(k == 0), stop=False), reads=['srep', 'aw%d' % (cgi % 2)], writes=[pt], inc=False)
                S.op('pe', lambda e, pv=pv, ab=ab: e.matmul(pv, ones[0:1, 0:128], ab[0:1, :], start=False, stop=True), reads=['ab%d' % (cgi % 2), 'c'], writes=[pt])
                cs = slice(hh * 512, hh * 512 + 512)
                if kind == 0:
                    S.op('dve', lambda e, j=j, pv=pv, cs=cs: e.tensor_copy(MSa[:, j, 1, cs], pv), reads=[pt], writes=['MS'])
                elif kind == 1:
                    S.op('dve', lambda e, j=j, pv=pv, cs=cs: e.scalar_tensor_tensor(MSa[:, j, 0, cs], pv, 1.0, nrow[:, 0, cs], ALU.add, ALU.mult), reads=[pt, 'rb_dst'], writes=['MS'])
                else:
                    S.op('dve', lambda e, j=j, pv=pv, cs=cs: e.tensor_tensor(MSg[:, j, cs], pv, nrow[:, 1, cs], ALU.mult), reads=[pt, 'rb_dst'], writes=['MS'])
        S.barrier()

    def p_small(l):
        A.reset()
        rowt = A.f32(512)
        pv = PS[3][:, 0:512]
        row_bcast(dtb[:], ssd_dtb[l], 16, rowt, pv)
        row_bcast(aneg[:], ssd_alog[l], 16, rowt, pv)
        S.op('act', lambda e: e.activation(aneg[:], aneg[:], AF.Exp), reads=['rb_dst'], writes=['rb_dst'])
        S.op('dve', lambda e: e.tensor_scalar(aneg[:], aneg[:], -1.0, None, ALU.mult), reads=['rb_dst'], writes=['rb_dst'])
        row_bcast(dsk[:], ssd_d[l], 8, rowt, pv)
        row_bcast(ssdn[:], ssd_norm[l], 512, rowt, pv)
        row_bcast(hgn[:], hg_norm[l], 64, rowt, pv)
        for w_ in range(4):
            S.dma('sp', scw[:, :, w_], ssd_cw[l, w_:w_ + 1, :].rearrange("o (k p) -> p (o k)", p=128), writes=['scw'], allow_slow_non_contiguous=True)
            S.dma('sp', fcw[:, :, w_], ffn_cw[l, w_:w_ + 1, :].rearrange("o (k p) -> p (o k)", p=128), writes=['fcw'], allow_slow_non_contiguous=True)
        mk = A.f32(64)
        S.dma('sp', mk, namask, writes=['mk'])
        for p in range(2):
            S.dma('sp', TBL[:, p, :], rpbT[l, p], writes=['TBL'])
            S.op('dve', lambda e, p=p: e.tensor_tensor(TBL[:, p, :].rearrange("q (r c) -> q r c", c=64), TBL[:, p, :].rearrange("q (r c) -> q r c", c=64),
                                                      bc(mk.unsqueeze(1), [128, 15, 64]), ALU.add), reads=['TBL', 'mk'], writes=['TBL'])
        if dbg:
            S.dma('pool', DBG[:, 0:88], fcw[:].rearrange("p k w -> p (k w)"), reads=['fcw'])
            S.dma('pool', DBG[:, 88:112], scw[:].rearrange("p k w -> p (k w)"), reads=['scw'])
        S.barrier()

    def p_normT(l, tiles=None):
        A.reset(MSA_N)
        hb = [A.f32(D) for _ in range(2)]
        tmp = A.f32(D)
        sq = A.f32(D)
        ub = [A.bf16(D) for _ in range(2)]
        utg = [A.bf16(8 * 512) for _ in range(2)]
        st = A.f32(4)
        gi = 0
        for b in range(NB):
            groups = [(0, 2)] + [(2 + 4 * m, 4) for m in range(8)]
            for (t0, nt) in groups:
                ug = utg[gi % 2]
                ugv = ug.rearrange("p (k c) -> p k c", k=8)
                for ti in range(nt):
                    j = t0 + ti
                    jj = b * 34 + j
                    mj = 2 if j < 2 else b
                    h = hb[jj % 2]
                    u = ub[jj % 2]
                    ht, utg_t = 'h%d' % (jj % 2), 'u%d' % (jj % 2)
                    S.dma('sp', h, Hs[b, j * 128:(j + 1) * 128, :], writes=[ht])
                    S.op('act', lambda e, h=h: e.activation(sq, h, AF.Square, accum_out=st[:, 0:1]), reads=[ht], writes=['sq', 'st'])
                    rstd_from_ss(st[:, 0:1], st[:, 1:2], D, ['st'], ['st1'])
                    S.op('dve', lambda e, h=h, mj=mj: e.scalar_tensor_tensor(tmp, h, st[:, 1:2], MSa[:, mj, 0, :], ALU.mult, ALU.mult), reads=[ht, 'st1'], writes=['tmp'])
                    S.op('pool', lambda e, u=u, mj=mj: e.tensor_tensor(u, tmp, MSa[:, mj, 1, :], ALU.add), reads=['tmp'], writes=[utg_t])
                    pv = PS[jj % 2].bitcast(BF16)[:, 0:1024].rearrange("p (k c) -> p k c", k=8)
                    pt = 'tp%d' % (jj % 2)
                    for k in range(8):
                        S.op('pe', lambda e, pv=pv, u=u, k=k: e.transpose(pv[:, k, :], u[:, k * 128:(k + 1) * 128], ident[:]), reads=[utg_t, 'c'], writes=[pt], inc=(k == 7))
                    S.op('act', lambda e, pv=pv, ugv=ugv, ti=ti: e.activation(ugv[:, :, ti * 128:(ti + 1) * 128], pv, AF.Copy), reads=[pt], writes=['ug%d' % (gi % 2)])
                n = nt * 128
                c0 = ucol(t0 * 128)
                S.dma('pool', UT[b].rearrange("(k p) c -> p k c", p=128)[:, :, c0:c0 + n], ugv[:, :, 0:n], reads=['ug%d' % (gi % 2)])
                gi += 1
        S.barrier()

    WIN_MAP = [(0, 512, 768), (768, 1296, 256), (1024, 1552, 256), (1280, 2064, 256), (1536, 2320, 512),
               (2048, 0, 512), (2560, 3088, 256), (2816, 1280, 16), (2832, 1808, 256), (3088, 2832, 256)]

    def p_inproj(l):
        A.reset()
        win = A.bf16(8 * 3344).rearrange("p (k c) -> p k c", k=8)
        stg = [A.f32(4096) for _ in range(2)]
        wsrc = w_in[l].rearrange("(k p) c -> p k c", p=128)
        for (dc, sc, w) in WIN_MAP:
            load_cast(win[:, :, dc:dc + w], wsrc[:, :, sc:sc + w], 8, w, stg, 'win')
        S.barrier()
        A.off -= 2 * 4096
        uts = [A.bf16(8 * 514) for _ in range(2)]
        xbcT = A.bf16(6 * 512).rearrange("p (k c) -> p k c", k=6)
        ctmp = A.f32(512)
        xtok = [A.bf16(640) for _ in range(2)]
        tokf = [A.f32(784) for _ in range(2)]
        tokb = [A.bf16(512) for _ in range(2)]
        nqk = [A.bf16(512) for _ in range(2)]
        qs = A.f32(2 * 512).rearrange("p (k c) -> p k c", k=2)
        sig = A.f32(512)
        t1 = A.f32(512)
        kk = A.f32(512)
        aext = A.f32(520)
        dd = A.f32(512)
        eD = A.f32(512)
        hqk = [A.bf16(2 * 512) for _ in range(2)]
        sc3 = [A.f32(48) for _ in range(2)]
        sc3t = A.f32(48)
        cnt = {'ps': 0, 'x': 0, 't': 0, 'n': 0, 'h': 0}

        def nps():
            i = cnt['ps'] % 4
            cnt['ps'] += 1
            return PS[i], 'ps%d' % i

        si = 0
        for b in range(NB):
            sts = [(0, 256)] + [(256 + 512 * m, 512) for m in range(8)]
            for (p0, n) in sts:
                ut = uts[si % 2]
                utv = ut.rearrange("p (k c) -> p k c", k=8)
                utag = 'ut%d' % (si % 2)
                si += 1
                c0 = ucol(p0)
                S.dma('sp', utv[:, :, 0:n + 2], UT[b].rearrange("(k p) c -> p k c", p=128)[:, :, c0 - 1:c0 + n + 1], writes=[utag])
                npc = n // 256
                for ch in range(6):
                    ps, pt = nps()
                    for pc in range(npc):
                        for k in range(8):
                            S.op('pe', lambda e, ps=ps, pc=pc, k=k, ch=ch, utv=utv: e.matmul(ps[:, pc * 512:pc * 512 + 258], win[:, k, ch * 128:(ch + 1) * 128], utv[:, k, pc * 256:pc * 256 + 258], start=(k == 0), stop=(k == 7)),
                                 reads=['win', utag], writes=[pt], inc=(k == 7 and pc == npc - 1))
                    pv = ps.rearrange("p (a c) -> p a c", c=512)[:, 0:npc, :]
                    cv = ctmp[:, 0:n].rearrange("p (a c) -> p a c", c=256)
                    S.op('act', lambda e, pv=pv, cv=cv, ch=ch: e.activation(cv, pv[:, :, 0:256], AF.Copy, scale=scw[:, ch, 0:1]), reads=[pt, 'scw'], writes=['ctmp'])
                    S.op('dve', lambda e, pv=pv, cv=cv, ch=ch: e.scalar_tensor_tensor(cv, pv[:, :, 1:257], scw[:, ch, 1:2], cv, ALU.mult, ALU.add), reads=[pt, 'ctmp'], writes=['ctmp'])
                    S.op('dve', lambda e, pv=pv, cv=cv, ch=ch: e.scalar_tensor_tensor(cv, pv[:, :, 2:258], scw[:, ch, 2:3], cv, ALU.mult, ALU.add), reads=[pt, 'ctmp'], writes=['ctmp'])
                    S.op('act', lambda e, ch=ch, n=n: e.activation(xbcT[:, ch, 0:n], ctmp[:, 0:n], AF.Silu, bias=scw[:, ch, 3:4]), reads=['ctmp'], writes=['xbcT%d' % ch])
                S.dma('pool', BCs[b].rearrange("a p t -> p a t")[:, :, p0:p0 + n], xbcT[:, 4:6, 0:n], reads=['xbcT4', 'xbcT5'])
                for s_ in range(n // 128):
                    pos = p0 + s_ * 128
                    ps, pt = nps()
                    pvb = ps.bitcast(BF16)[:, 0:640].rearrange("p (k c) -> p k c", k=5)
                    for k in range(5):
                        S.op('pe', lambda e, pvb=pvb, k=k, s_=s_: e.transpose(pvb[:, k, :], xbcT[:, k, s_ * 128:(s_ + 1) * 128], ident[:]), reads=['xbcT%d' % k, 'c'], writes=[pt], inc=(k == 4))
                    xt = xtok[cnt['x'] % 2]
                    xtag = 'xtok%d' % (cnt['x'] % 2)
                    cnt['x'] += 1
                    S.op('act', lambda e, xt=xt, ps=ps: e.activation(xt, ps.bitcast(BF16)[:, 0:640], AF.Copy), reads=[pt], writes=[xtag])
                    S.dma('pool', XB[b, pos:pos + 128, :], xt, reads=[xtag])
                    tf = tokf[cnt['t'] % 2]
                    tb_ = tokb[cnt['t'] % 2]
                    ttag = 'tok%d' % (cnt['t'] % 2)
                    cnt['t'] += 1
                    lhs = lambda k, s_=s_, utv=utv: utv[:, k, 1 + s_ * 128:1 + (s_ + 1) * 128]
                    for (wc, w, dst, dtag) in ((2048, 512, tf[:, 0:512], ttag + 'f'), (2560, 272, tf[:, 512:784], ttag + 'f'), (2832, 512, tb_, ttag + 'b')):
                        ps, pt = nps()
                        for k in range(8):
                            S.op('pe', lambda e, ps=ps, k=k, wc=wc, w=w, lhs=lhs: e.matmul(ps[:, 0:w], lhs(k), win[:, k, wc:wc + w], start=(k == 0), stop=(k == 7)), reads=['win', utag], writes=[pt], inc=(k == 7))
                        S.op('act', lambda e, ps=ps, w=w, dst=dst: e.activation(dst, ps[:, 0:w], AF.Copy), reads=[pt], writes=[dtag])
                    S.dma('pool', TOK[b, pos:pos + 128, :], tf, reads=[ttag + 'f'])
                    S.dma('pool', TB[b, pos:pos + 128, :], tb_, reads=[ttag + 'b'])
                for ch in range(4):
                    ps, pt = nps()
                    for k in range(8):
                        S.op('pe', lambda e, ps=ps, k=k, ch=ch, utv=utv, n=n: e.matmul(ps[:, 0:n], win[:, k, 768 + ch * 128:768 + (ch + 1) * 128], utv[:, k, 1:1 + n], start=(k == 0), stop=(k == 7)), reads=['win', utag], writes=[pt], inc=(k == 7))
                    nb_ = nqk[cnt['n'] % 2]
                    ntag = 'nqk%d' % (cnt['n'] % 2)
                    cnt['n'] += 1
                    S.op('act', lambda e, ps=ps, nb_=nb_, n=n, ch=ch: e.activation(nb_[:, 0:n], ps[:, 0:n], AF.Copy, scale=(0.125 if ch < 2 else 1.0)), reads=[pt], writes=[ntag])
                    S.dma('pool', NAQK[b, ch, :, p0:p0 + n], nb_[:, 0:n], reads=[ntag])
                for ch in range(2):
                    ps, pt = nps()
                    for k in range(8):
                        S.op('pe', lambda e, ps=ps, k=k, ch=ch, utv=utv, n=n: e.matmul(ps[:, 0:n], win[:, k, 1280 + ch * 128:1280 + (ch + 1) * 128], utv[:, k, 1:1 + n], start=(k == 0), stop=(k == 7)), reads=['win', utag], writes=[pt], inc=(k == 7))
                    S.op('act', lambda e, ps=ps, ch=ch, n=n: e.activation(qs[:, ch, 0:n], ps[:, 0:n], AF.Silu), reads=[pt], writes=['qs%d' % ch])
                nch = n // 32
                for d_ in range(2):
                    for pr in range(2):
                        ps, pt = nps()
                        fc = 1536 + (d_ * 2 + pr) * 128
                        for k in range(8):
                            S.op('pe', lambda e, ps=ps, k=k, fc=fc, utv=utv, n=n: e.matmul(ps[:, 0:n], win[:, k, fc:fc + 128], utv[:, k, 1:1 + n], start=(k == 0), stop=(k == 7)), reads=['win', utag], writes=[pt], inc=(k == 7))
                        S.op('act', lambda e, ps=ps, n=n: e.activation(sig[:, 0:n], ps[:, 0:n], AF.Sigmoid), reads=[pt], writes=['sig'])
                        S.op('dve', lambda e, n=n, pr=pr: e.tensor_scalar(t1[:, 0:n], sig[:, 0:n], OML[:, pr, l:l + 1], LB[:, pr, l:l + 1], ALU.mult, ALU.add), reads=['sig'], writes=['t1'])
                        S.op('act', lambda e, n=n: e.activation(t1[:, 0:n], t1[:, 0:n], AF.Ln), reads=['t1'], writes=['t1'])
                        S.op('dve', lambda e, n=n, pr=pr: e.tensor_scalar(kk[:, 0:n], sig[:, 0:n], NOML[:, pr, l:l + 1], OML[:, pr, l:l + 1], ALU.mult, ALU.add), reads=['sig'], writes=['kk'])
                        S.op('dve', lambda e: e.memset(aext[:, 0:1], 0.0), writes=['aext'])
                        S.op('dve', lambda e, n=n: e.tensor_tensor_scan(aext[:, 1:n + 1], ones[:, 0:n], t1[:, 0:n], 0.0, ALU.mult, ALU.add), reads=['t1', 'c'], writes=['aext'])
                        off = 1 if d_ == 0 else 0
                        Pv = aext[:, off:off + n].rearrange("p (c t) -> p c t", t=32)
                        pref = Pv[:, :, 16:17]
                        ddv = dd[:, 0:n].rearrange("p (c t) -> p c t", t=32)
                        if d_ == 0:
                            S.op('dve', lambda e, Pv=Pv, pref=pref, ddv=ddv, nch=nch: e.tensor_tensor(ddv, Pv, bc(pref, [128, nch, 32]), ALU.subtract), reads=['aext'], writes=['dd'])
                        else:
                            S.op('dve', lambda e, Pv=Pv, pref=pref, ddv=ddv, nch=nch: e.tensor_tensor(ddv, bc(pref, [128, nch, 32]), Pv, ALU.subtract), reads=['aext'], writes=['dd'])
                        hb_ = hqk[cnt['h'] % 2]
                        hbv = hb_.rearrange("p (a c) -> p a c", a=2)
                        s3 = sc3[cnt['h'] % 2]
                        htag = 'hqk%d' % (cnt['h'] % 2)
                        cnt['h'] += 1
                        S.op('act', lambda e, n=n: e.activation(eD[:, 0:n], dd[:, 0:n], AF.Exp), reads=['dd'], writes=['eD'])
                        S.op('dve', lambda e, n=n, pr=pr, hbv=hbv: e.tensor_tensor(hbv[:, 0, 0:n], qs[:, pr, 0:n], eD[:, 0:n], ALU.mult), reads=['eD', 'qs%d' % pr], writes=[htag])
                        S.op('act', lambda e, n=n: e.activation(eD[:, 0:n], dd[:, 0:n], AF.Exp, scale=-1.0), reads=['dd', htag], writes=['eD'])
                        S.op('dve', lambda e, n=n, hbv=hbv: e.tensor_tensor(hbv[:, 1, 0:n], kk[:, 0:n], eD[:, 0:n], ALU.mult), reads=['eD', 'kk'], writes=[htag])
                        lo = aext[:, 0:n].rearrange("p (c t) -> p c t", t=32)[:, :, 0:1]
                        hi = aext[:, 1:n + 1].rearrange("p (c t) -> p c t", t=32)[:, :, 31:32]
                        s3v = sc3t[:, 0:nch * 3].rearrange("p (c t) -> p c t", t=3)
                        if d_ == 0:
                            trip = ((pref, lo), (hi, lo), (hi, pref))
                        else:
                            trip = ((hi, pref), (hi, lo), (pref, lo))
                        for i3, (a_, b_) in enumerate(trip):
                            S.op('dve', lambda e, i3=i3, a_=a_, b_=b_, s3v=s3v: e.tensor_tensor(s3v[:, :, i3:i3 + 1], a_, b_, ALU.subtract), reads=['aext'], writes=['sc3t'])
                        S.op('act', lambda e, s3=s3, nch=nch: e.activation(s3[:, 0:nch * 3], sc3t[:, 0:nch * 3], AF.Exp), reads=['sc3t'], writes=[htag + 's'])
                        S.dma('pool', HGQK[b, d_, :, pr, :, p0:p0 + n].rearrange("a p t -> p a t"), hbv[:, :, 0:n], reads=[htag])
                        S.dma('pool', HGS[b, d_, pr, :, p0 // 32:p0 // 32 + nch, :], s3[:, 0:nch * 3].rearrange("p (c t) -> p c t", t=3), reads=[htag + 's'])
        S.barrier()

    def p_ssd(l, d_):
        import os
        STG = int(os.environ.get('K_SSD_STAGE', '99'))
        NCH = int(os.environ.get('K_SSD_NCH', '99'))
        A.reset()
        TRI = TRIf if d_ == 0 else TRIb
        MSK = MGT if d_ == 0 else MLT
        xb = [A.bf16(640) for _ in range(2)]
        bct = [A.bf16(256) for _ in range(2)]
        dtr = [A.f32(16) for _ in range(2)]
        zt = [A.f32(512) for _ in range(2)]
        yft = [A.f32(512) for _ in range(2)]
        sm = A.f32(64)
        xdt = A.bf16(512)
        xw = A.bf16(512)
        lh = A.f32(1024)
        Ee = A.f32(1024)
        scm = A.f32(256)
        MT = A.bf16(1024)
        tmpy = A.f32(512)
        yd = [A.f32(512) for _ in range(2)]
        Hm = A.f32(256)
        Hb = A.bf16(256)
        tmph = A.f32(256)
        sq = A.f32(512)
        st = A.f32(4)
        ob = [A.bf16(512) for _ in range(2)]
        Cm = [[A.bf16(128) for _ in range(2)] for _ in range(2)]
        for i in range(2):
            for g in range(2):
                S.op('pool', lambda e, i=i, g=g: e.memset(Cm[i][g], 0.0), writes=['Cm%d' % i])
        dt_, loga, cum, tot, ecum, wv, etot = [sm[:, i * 8:(i + 1) * 8] for i in range(7)]
        PSs, PSc, PSy, PSi = PS[0], PS[1], PS[2], PS[3]
        ci = 0
        for b in range(NB):
            S.op('dve', lambda e: e.memset(Hm, 0.0), reads=['Hb'], writes=['Hm'])
            S.op('pool', lambda e: e.memset(Hb, 0.0), reads=['Hm'], writes=['Hb'])
            order = list(range(34)) if d_ == 0 else [1, 0] + list(range(33, 1, -1))
            for j in order[:NCH]:
                pos = j * 128
                q = ci % 2
                ci += 1
                x_, bc_, dr = xb[q], bct[q].rearrange("p (a t) -> p a t", a=2), dtr[q]
                lt = 'ld%d' % q
                S.dma('sp', x_, XB[b, pos:pos + 128, :], writes=[lt + 'x'])
                S.dma('sp', bc_, BCs[b].rearrange("a p t -> p a t")[:, :, pos:pos + 128], writes=[lt + 'b'])
                S.dma('sp', dr, TOK[b, pos:pos + 128, 768:784], writes=[lt + 'd'])
                if d_ == 1:
                    S.dma('sp', zt[q], TOK[b, pos:pos + 128, 0:512], writes=[lt + 'z'])
                    S.dma('sp', yft[q], YF[b, pos:pos + 128, :], writes=[lt + 'y'])
                S.op('dve', lambda e, dr=dr: e.tensor_tensor(dt_, dr[:, d_ * 8:d_ * 8 + 8], dtb[:, d_ * 8:d_ * 8 + 8], ALU.add), reads=[lt + 'd'], writes=['dt'])
                S.op('act', lambda e: e.activation(dt_, dt_, AF.Exp), reads=['dt'], writes=['dt'])
                S.op('act', lambda e: e.activation(dt_, dt_, AF.Ln, bias=1.0), reads=['dt'], writes=['dt'])
                S.op('dve', lambda e: e.tensor_tensor(loga, dt_, aneg[:, d_ * 8:d_ * 8 + 8], ALU.mult), reads=['dt'], writes=['loga'])
                S.op('dve', lambda e, x_=x_: e.tensor_tensor(xdt.rearrange("p (h c) -> p h c", h=8), x_[:, 0:512].rearrange("p (h c) -> p h c", h=8), bc(dt_.unsqueeze(2), [128, 8, 64]), ALU.mult), reads=[lt + 'x', 'dt'], writes=['xdt'])
                if STG < 1:
                    continue
                S.op('pe', lambda e: e.matmul(PSc[:, 512:520], TRI[:], loga, start=True, stop=True), reads=['loga', 'c'], writes=['psc2'], inc=False)
                S.op('pe', lambda e: e.matmul(PSc[:, 520:528], ones[:, 0:128], loga, start=True, stop=True), reads=['loga', 'c'], writes=['psc2'])
                S.op('dve', lambda e: e.tensor_copy(sm[:, 16:32], PSc[:, 512:528]), reads=['psc2'], writes=['cum'])
                S.op('act', lambda e: e.activation(ecum, cum, AF.Exp), reads=['cum'], writes=['ecum'])
                S.op('dve', lambda e: e.tensor_tensor(wv, tot, cum, ALU.subtract), reads=['cum'], writes=['wv'])
                S.op('act', lambda e: e.activation(wv, wv, AF.Exp), reads=['wv'], writes=['wv'])
                S.op('act', lambda e: e.activation(etot, tot, AF.Exp), reads=['cum'], writes=['etot'])
                S.op('pool', lambda e: e.tensor_tensor(xw.rearrange("p (h c) -> p h c", h=8), xdt.rearrange("p (h c) -> p h c", h=8), bc(wv.unsqueeze(2), [128, 8, 64]), ALU.mult), reads=['xdt', 'wv'], writes=['xw'])
                if STG < 2:
                    continue
                lhv = lh.rearrange("p (h s) -> p h s", h=8)
                S.op('pool', lambda e: e.tensor_tensor(lhv, bc(MSK[:].unsqueeze(1), [128, 8, 128]), bc(loga.unsqueeze(2), [128, 8, 128]), ALU.mult), reads=['loga', 'c'], writes=['lh'])
                for h in range(8):
                    S.op('pe', lambda e, h=h: e.matmul(PSs[:, h * 128:(h + 1) * 128], lhv[:, h, :], TRI[:], start=True, stop=True), reads=['lh', 'c'], writes=['pss'], inc=(h == 7))
                for hf in range(2):
                    S.op('act', lambda e, hf=hf: e.activation(Ee[:, hf * 512:(hf + 1) * 512], PSs[:, hf * 512:(hf + 1) * 512], AF.Exp), reads=['pss'], writes=['Ee'])
                if STG < 3:
                    continue
                for g in range(2):
                    S.op('pool', lambda e, g=g, bc_=bc_: e.tensor_copy(Cm[q][g][g * 64:(g + 1) * 64, :], bc_[g * 64:(g + 1) * 64, 1, :]), reads=[lt + 'b'], writes=['Cm%d' % q])
                for g in range(2):
                    S.op('pe', lambda e, g=g, bc_=bc_: e.matmul(PSc[:, g * 128:(g + 1) * 128], bc_[:, 0, :], Cm[q][g], start=True, stop=True), reads=[lt + 'b', 'Cm%d' % q], writes=['psc'], inc=(g == 1))
                if os.environ.get('K_SUB', 'z') < 'b':
                    continue
                S.op('dve', lambda e: e.tensor_tensor(scm.rearrange("p (g t) -> p g t", g=2), PSc[:, 0:256].rearrange("p (g t) -> p g t", g=2), bc(TRI[:].unsqueeze(1), [128, 2, 128]), ALU.mult), reads=['psc', 'c'], writes=['scm'])
                S.op('dve', lambda e: e.tensor_tensor(MT.rearrange("p (g h t) -> p g h t", g=2, h=4), Ee.rearrange("p (g h t) -> p g h t", g=2, h=4),
                                                      bc(scm.rearrange("p (g t) -> p g t", g=2).unsqueeze(2), [128, 2, 4, 128]), ALU.mult), reads=['Ee', 'scm'], writes=['MT'])
                if os.environ.get('K_SUB', 'z') < 'c':
                    continue
                for h in range(8):
                    S.op('pe', lambda e, h=h: e.matmul(PSy[:, h * 64:(h + 1) * 64], MT[:, h * 128:(h + 1) * 128], xdt[:, h * 64:(h + 1) * 64], start=True, stop=True), reads=['MT', 'xdt'], writes=['psy'], inc=(h == 7))
                if STG < 4:
                    continue
                for g in range(2):
                    S.op('pe', lambda e, g=g: e.matmul(PSi[:, g * 256:(g + 1) * 256], Cm[q][g], Hb, start=True, stop=True), reads=['Cm%d' % q, 'Hb'], writes=['psi'], inc=(g == 1))
                ydq = yd[q]
                S.op('dve', lambda e: e.tensor_tensor(tmpy.rearrange("p (h c) -> p h c", h=8), PSi[:, 0:512].rearrange("p (h c) -> p h c", h=8), bc(ecum.unsqueeze(2), [128, 8, 64]), ALU.mult), reads=['psi', 'ecum'], writes=['tmpy'])
                S.op('dve', lambda e, ydq=ydq: e.tensor_tensor(ydq, tmpy, PSy[:, 0:512], ALU.add), reads=['tmpy', 'psy'], writes=['yd%d' % q])
                if STG < 5:
                    continue
                for g in range(2):
                    S.op('pe', lambda e, g=g, x_=x_: e.matmul(PSi[:, 512 + g * 256:512 + (g + 1) * 256], x_[:, 512:640], xw[:, g * 256:(g + 1) * 256], start=True, stop=True), reads=[lt + 'x', 'xw'], writes=['psu'], inc=(g == 1))
                for g in range(2):
                    rs_ = slice(g * 64, (g + 1) * 64)
                    S.op('dve', lambda e, g=g, rs_=rs_: e.tensor_tensor(tmph[rs_, :].rearrange("p (h c) -> p h c", h=4), Hm[rs_, :].rearrange("p (h c) -> p h c", h=4), bc(etot[rs_, g * 4:(g + 1) * 4].unsqueeze(2), [64, 4, 64]), ALU.mult), reads=['Hm', 'etot'], writes=['tmph'])
                    S.op('dve', lambda e, g=g, rs_=rs_: e.tensor_tensor(Hm[rs_, :], tmph[rs_, :], PSi[rs_, 512 + g * 256:512 + (g + 1) * 256], ALU.add), reads=['tmph', 'psu'], writes=['Hm'])
                S.op('act', lambda e: e.activation(Hb, Hm, AF.Copy), reads=['Hm'], writes=['Hb'])
                if d_ == 0:
                    S.dma('pool', YF[b, pos:pos + 128, :], ydq, reads=['yd%d' % q])
                else:
                    z_, yf_, o_ = zt[q], yft[q], ob[q]
                    S.op('pool', lambda e, ydq=ydq, yf_=yf_: e.tensor_tensor(ydq, ydq, yf_, ALU.add), reads=['yd%d' % q, lt + 'y'], writes=['yd%d' % q])
                    S.op('dve', lambda e, x_=x_: e.tensor_tensor(tmpy.rearrange("p (h c) -> p h c", h=8), x_[:, 0:512].rearrange("p (h c) -> p h c", h=8), bc(dsk[:].unsqueeze(2), [128, 8, 64]), ALU.mult), reads=[lt + 'x'], writes=['tmpy'])
                    S.op('pool', lambda e, ydq=ydq: e.tensor_tensor(ydq, ydq, tmpy, ALU.add), reads=['tmpy', 'yd%d' % q], writes=['yd%d' % q])
                    S.op('act', lambda e, z_=z_: e.activation(z_, z_, AF.Silu), reads=[lt + 'z'], writes=[lt + 'z'])
                    S.op('dve', lambda e, ydq=ydq, z_=z_: e.tensor_tensor(ydq, ydq, z_, ALU.mult), reads=['yd%d' % q, lt + 'z'], writes=['yd%d' % q])
                    S.op('act', lambda e, ydq=ydq: e.activation(sq, ydq, AF.Square, accum_out=st[:, 0:1]), reads=['yd%d' % q], writes=['sq', 'st'])
                    rstd_from_ss(st[:, 0:1], st[:, 1:2], 512, ['st'], ['st1'])
                    S.op('dve', lambda e, ydq=ydq, o_=o_: e.scalar_tensor_tensor(o_, ydq, st[:, 1:2], ssdn[:], ALU.mult, ALU.mult), reads=['yd%d' % q, 'st1'], writes=['ob%d' % q])
                    S.dma('pool', MIX[b, pos:pos + 128, 0:512], o_, reads=['ob%d' % q])
        S.barrier()

    def p_hg(l, d_):
        import os
        A.reset()
        BM = BMf if d_ == 0 else BMb
        qk = [A.bf16(4 * 128) for _ in range(2)]
        vt = [A.bf16(256) for _ in range(2)]
        gt = [A.f32(256) for _ in range(2)]
        oft = [A.f32(256) for _ in range(2)]
        s3 = [A.f32(24) for _ in range(2)]
        qc = [A.bf16(2 * 128) for _ in range(4)]
        qh = [A.bf16(2 * 128) for _ in range(2)]
        ktc = [A.bf16(256) for _ in range(4)]
        Ssh = [A.bf16(64) for _ in range(4)]
        ktok = A.bf16(256)
        PT = A.bf16(512)
        Sm = A.f32(128)
        Ss = A.bf16(128)
        tmpu = A.f32(256)
        osb = [A.f32(256) for _ in range(2)]
        sq = A.f32(256)
        ssum = A.f32(8)
        o2 = A.f32(256)
        ob = [A.bf16(256) for _ in range(2)]
        for i in range(4):
            S.op('pool', lambda e, i=i: e.memset(qc[i], 0.0), writes=['qc%d' % i])
        for i in range(2):
            S.op('pool', lambda e, i=i: e.memset(qh[i], 0.0), writes=['qh%d' % i])
        for i in range(4):
            S.op('pool', lambda e, i=i: e.memset(Ssh[i], 0.0), writes=['Ss'])
        Smv = Sm.rearrange("p (a v) -> p a v", a=2)
        Ssv = Ss.rearrange("p (a v) -> p a v", a=2)
        ci = 0
        for b in range(NB):
            S.op('dve', lambda e: e.memset(Sm, 0.0), reads=['Ss'], writes=['Sm'])
            order = list(range(34)) if d_ == 0 else [1, 0] + list(range(33, 1, -1))
            for j in order:
                pos = j * 128
                q = ci % 2
                ci += 1
                lt = 'ld%d' % q
                qkv = qk[q].rearrange("p (a r t) -> p a r t", a=2, r=2)
                S.dma('sp', qk[q].rearrange("p (m t) -> p m t", t=128), HGQK[b, d_, :, :, :, pos:pos + 128].rearrange("a r p t -> p (a r) t"), writes=[lt + 'q'])
                S.dma('sp', vt[q], TB[b, pos:pos + 128, 256:512], writes=[lt + 'v'])
                s3v = s3[q].rearrange("p (r c t) -> p r c t", r=2, c=4)
                S.dma('sp', s3v, HGS[b, d_, :, :, pos // 32:pos // 32 + 4, :].rearrange("r p c t -> p r c t"), writes=[lt + 's'])
                if d_ == 1:
                    S.dma('sp', gt[q], TOK[b, pos:pos + 128, 512:768], writes=[lt + 'g'])
                    S.dma('sp', oft[q], OF[b, pos:pos + 128, :], writes=[lt + 'o'])
                pkt = PS[0].bitcast(BF16)[:, 1024:1280]
                for r in range(2):
                    S.op('pe', lambda e, r=r, qkv=qkv: e.transpose(pkt[:, r * 128:(r + 1) * 128], qkv[:, 1, r, :], ident[:]), reads=[lt + 'q', 'c'], writes=['pkt'], inc=(r == 1))
                S.op('act', lambda e: e.activation(ktok, pkt, AF.Copy), reads=['pkt'], writes=['ktok'])
                for c_ in range(4):
                    S.op('dve' if c_ % 2 == 0 else 'pool', lambda e, c_=c_: e.tensor_scalar(ktc[c_], ktok, RM[:, c_:c_ + 1], None, ALU.mult), reads=['ktok', 'c'], writes=['ktc%d' % c_])
                for hh in range(2 if os.environ.get('K_HGX', '') != 'noqh' else 0):
                    S.op('pool', lambda e, hh=hh, qkv=qkv: e.tensor_copy(qh[hh].rearrange("p (r t) -> p r t", r=2)[hh * 64:(hh + 1) * 64, :, :], qkv[hh * 64:(hh + 1) * 64, 0, :, :]), reads=[lt + 'q'], writes=['qh%d' % hh])
                if os.environ.get('K_HG', 'z') < 'b':
                    continue
                for h in range(4):
                    r, hr = h // 2, slice((h % 2) * 64, (h % 2) * 64 + 64)
                    S.op('pe', lambda e, h=h, r=r, qkv=qkv: e.matmul(PS[0][:, h * 128:(h + 1) * 128], qkv[:, 1, r, :], qh[h % 2].rearrange("p (r t) -> p r t", r=2)[:, r, :], start=True, stop=True), reads=[lt + 'q', 'qh%d' % (h % 2)], writes=['psS'], inc=(h == 3))
                S.op('dve', lambda e: e.tensor_tensor(PT.rearrange("p (h t) -> p h t", h=4), PS[0][:, 0:512].rearrange("p (h t) -> p h t", h=4), bc(BM[:].unsqueeze(1), [128, 4, 128]), ALU.mult), reads=['psS', 'c'], writes=['PT'])
                cords = (0, 1, 2, 3) if d_ == 0 else (3, 2, 1, 0)
                for c_ in range(4):
                    S.op('pool', lambda e, c_=c_, qkv=qkv: e.tensor_copy(qc[c_].rearrange("p (r t) -> p r t", r=2)[:, :, c_ * 32:(c_ + 1) * 32], qkv[:, 0, :, c_ * 32:(c_ + 1) * 32]), reads=[lt + 'q'], writes=['qc%d' % c_])
                POh = [PS[1][:, 512:576], PS[2][:, 0:64], PS[2][:, 512:576], PS[3][:, 0:64]]
                if os.environ.get('K_HG', 'z') < 'c':
                    continue
                for ic, c_ in enumerate(cords):
                    for h in range(4):
                        r, hr = h // 2, slice((h % 2) * 64, (h % 2) * 64 + 64)
                        S.op('dve', lambda e, h=h, r=r, hr=hr, c_=c_, s3v=s3v: e.tensor_scalar(Ssh[h][hr, :], Smv[hr, r, :], s3v[hr, r, c_, 0:1], None, ALU.mult), reads=['Sm', lt + 's'], writes=['Ss'])
                    for h in range(4):
                        r, hr = h // 2, slice((h % 2) * 64, (h % 2) * 64 + 64)
                        ov = POh[h]
                        if ic == 0:
                            S.op('pe', lambda e, h=h, ov=ov: e.matmul(ov, PT[:, h * 128:(h + 1) * 128], vt[q][:, h * 64:(h + 1) * 64], start=True, stop=False), reads=['PT', lt + 'v'], writes=['pso%d' % h], inc=False)
                        S.op('pe', lambda e, h=h, r=r, ov=ov, c_=c_, ic=ic: e.matmul(ov, qc[c_].rearrange("p (r t) -> p r t", r=2)[:, r, :], Ssh[h], start=False, stop=(ic == 3)), reads=['qc%d' % c_, 'Ss'], writes=['pso%d' % h], inc=(h == 3))
                    for h in range(4):
                        r = h // 2
                        S.op('pe', lambda e, h=h, r=r, c_=c_: e.matmul(PS[1][:, h * 64:(h + 1) * 64], ktc[c_][:, r * 128:(r + 1) * 128], vt[q][:, h * 64:(h + 1) * 64], start=True, stop=True), reads=['ktc%d' % c_, lt + 'v'], writes=['psU'], inc=(h == 3))
                    for h in range(4):
                        r, hr = h // 2, slice((h % 2) * 64, (h % 2) * 64 + 64)
                        S.op('dve', lambda e, h=h, r=r, hr=hr, c_=c_, s3v=s3v: e.tensor_scalar(tmpu[hr, h * 64:(h + 1) * 64], PS[1][hr, h * 64:(h + 1) * 64], s3v[hr, r, c_, 2:3], None, ALU.mult), reads=['psU', lt + 's'], writes=['tmpu'])
                        S.op('dve', lambda e, h=h, r=r, hr=hr, c_=c_, s3v=s3v: e.scalar_tensor_tensor(Smv[hr, r, :], Smv[hr, r, :], s3v[hr, r, c_, 1:2], tmpu[hr, h * 64:(h + 1) * 64], ALU.mult, ALU.add), reads=['tmpu', 'Sm', lt + 's'], writes=['Sm'])
                o_ = osb[q]
                if d_ == 0:
                    for h in range(4):
                        S.op('act', lambda e, o_=o_, h=h: e.activation(o_[:, h * 64:(h + 1) * 64], POh[h], AF.Copy), reads=['pso%d' % h], writes=['osb%d' % q])
                    S.dma('pool', OF[b, pos:pos + 128, :], o_, reads=['osb%d' % q])
                else:
                    for h in range(4):
                        S.op('dve', lambda e, o_=o_, h=h: e.tensor_tensor(o_[:, h * 64:(h + 1) * 64], POh[h], oft[q][:, h * 64:(h + 1) * 64], ALU.add), reads=['pso%d' % h, lt + 'o'], writes=['osb%d' % q])
                    S.op('act', lambda e, o_=o_: e.activation(sq, o_, AF.Square), reads=['osb%d' % q], writes=['sq'])
                    S.op('dve', lambda e: e.tensor_reduce(ssum[:, 0:4], sq.rearrange("p (h v) -> p h v", h=4), AX.X, ALU.add), reads=['sq'], writes=['ssum'])
                    rstd_from_ss(ssum[:, 0:4], ssum[:, 4:8], 64, ['ssum'], ['ssum1'])
                    S.op('dve', lambda e, o_=o_: e.tensor_tensor(o2.rearrange("p (h v) -> p h v", h=4), o_.rearrange("p (h v) -> p h v", h=4), bc(ssum[:, 4:8].unsqueeze(2), [128, 4, 64]), ALU.mult), reads=['osb%d' % q, 'ssum1'], writes=['o2'])
                    S.op('pool', lambda e: e.tensor_tensor(o2.rearrange("p (h v) -> p h v", h=4), o2.rearrange("p (h v) -> p h v", h=4), bc(hgn[:].unsqueeze(1), [128, 4, 64]), ALU.mult), reads=['o2'], writes=['o2'])
                    S.op('act', lambda e: e.activation(gt[q], gt[q], AF.Silu), reads=[lt + 'g'], writes=[lt + 'g'])
                    S.op('dve', lambda e: e.tensor_tensor(ob[q], o2, gt[q], ALU.mult), reads=['o2', lt + 'g'], writes=['ob%d' % q])
                    S.dma('pool', MIX[b, pos:pos + 128, 768:1024], ob[q], reads=['ob%d' % q])
        S.barrier()

    def p_na(l):
        A.reset()
        KT = A.bf16(2 * T).rearrange("p (r t) -> p r t", r=2)
        QT = A.bf16(2 * T).rearrange("p (r t) -> p r t", r=2)
        V = A.bf16(34 * 256).rearrange("p (j c) -> p j c", j=34)
        bd = [A.bf16(128) for _ in range(2)]
        sS = A.f32(768)
        Pe = A.bf16(768)
        Po = A.bf16(896)
        PTs = [A.bf16(7 * 128) for _ in range(2)]
        mx = A.f32(4)
        onb = [A.bf16(128) for _ in range(2)]
        for i in range(2):
            S.op('pool', lambda e, i=i: e.memset(bd[i], 0.0), writes=['bd%d' % i])
        S.op('pool', lambda e: e.memset(Po, 0.0), writes=['Po'])
        ui = 0
        for b in range(NB):
            S.dma('sp', QT, NAQK[b, 0:2].rearrange("r p t -> p r t"), writes=['QT'])
            S.dma('sp', KT, NAQK[b, 2:4].rearrange("r p t -> p r t"), writes=['KT'])
            for j0 in range(0, 34, 8):
                j1 = min(34, j0 + 8)
                S.dma('sp', V[:, j0:j1, :], TB[b, j0 * 128:j1 * 128, 0:256].rearrange("(j p) c -> p j c", p=128), writes=['V'])
            units = [('c', i) for i in range(4)] + [('l', i) for i in range(64)]
            for (kind, i) in units:
                for pr in range(2):
                    u = ui % 2
                    ui += 1
                    qpos = i * 64 if kind == 'c' else 256 + i * 64
                    bdt = bd[u]
                    for hh in range(2):
                        rs_ = slice(hh * 64, hh * 64 + 64)
                        S.op('pool', lambda e, bdt=bdt, rs_=rs_, pr=pr, qpos=qpos: e.tensor_copy(bdt[rs_, rs_], QT[rs_, pr, qpos:qpos + 64]), reads=['QT'], writes=['bd%d' % u])
                    if kind == 'l':
                        r0 = min(max(i - 4, 0), 56)
                        off = r0 - i + 7
                        kpos = 256 + r0 * 64
                        S.op('pe', lambda e, bdt=bdt, pr=pr, kpos=kpos: e.matmul(PS[0][:, 0:512], bdt, KT[:, pr, kpos:kpos + 512], start=True, stop=True), reads=['bd%d' % u, 'KT'], writes=['psA'], inc=False)
                    S.op('pe', lambda e, bdt=bdt, pr=pr: e.matmul(PS[0][:, 512:768], bdt, KT[:, pr, 0:256], start=True, stop=True), reads=['bd%d' % u, 'KT'], writes=['psA'])
                    if kind == 'l':
                        S.op('dve', lambda e, pr=pr, off=off: e.tensor_tensor(sS[:, 0:512], PS[0][:, 0:512], TBL[:, pr, off * 64:off * 64 + 512], ALU.add), reads=['psA', 'TBL'], writes=['sS'])
                        S.op('act', lambda e: e.activation(sS[:, 512:768], PS[0][:, 512:768], AF.Copy), reads=['psA'], writes=['sS'])
                        sv = sS[:, 0:768]
                        odd = (r0 % 2 == 1)
                        if odd:
                            pdst = [Po[:, 64:576], Po[:, 640:896]]
                            ptiles = [Po[:, k * 128:(k + 1) * 128] for k in range(7)]
                            vt0 = (256 + (r0 - 1) * 64) // 128
                            vtl = [vt0 + k for k in range(5)] + [0, 1]
                            ptag = 'Po'
                        else:
                            pdst = [Pe[:, 0:512], Pe[:, 512:768]]
                            ptiles = [Pe[:, k * 128:(k + 1) * 128] for k in range(6)]
                            vt0 = (256 + r0 * 64) // 128
                            vtl = [vt0 + k for k in range(4)] + [0, 1]
                            ptag = 'Pe'
                    else:
                        S.op('act', lambda e: e.activation(sS[:, 512:768], PS[0][:, 512:768], AF.Copy), reads=['psA'], writes=['sS'])
                        sv = sS[:, 512:768]
                        pdst = [Pe[:, 512:768]]
                        ptiles = [Pe[:, 512:640], Pe[:, 640:768]]
                        vtl = [0, 1]
                        ptag = 'Pe'
                    S.op('dve', lambda e, sv=sv: e.tensor_reduce(mx[:, 0:1], sv, AX.X, ALU.max), reads=['sS'], writes=['mx'])
                    S.op('dve', lambda e: e.tensor_scalar(mx[:, 1:2], mx[:, 0:1], -1.0, None, ALU.mult), reads=['mx'], writes=['mx1'])
                    if kind == 'l':
                        S.op('act', lambda e, pdst=pdst: e.activation(pdst[0], sS[:, 0:512], AF.Exp, bias=mx[:, 1:2], accum_out=mx[:, 2:3]), reads=['sS', 'mx1'], writes=[ptag, 'mx2'])
                        S.op('act', lambda e, pdst=pdst: e.activation(pdst[1], sS[:, 512:768], AF.Exp, bias=mx[:, 1:2], accum_out=mx[:, 3:4]), reads=['sS', 'mx1'], writes=[ptag, 'mx3'])
                        S.op('dve', lambda e: e.tensor_tensor(mx[:, 2:3], mx[:, 2:3], mx[:, 3:4], ALU.add), reads=['mx2', 'mx3'], writes=['mx2'])
                    else:
                        S.op('act', lambda e, pdst=pdst: e.activation(pdst[0], sS[:, 512:768], AF.Exp, bias=mx[:, 1:2], accum_out=mx[:, 2:3]), reads=['sS', 'mx1'], writes=[ptag, 'mx2'])
                    S.op('dve', lambda e: e.reciprocal(mx[:, 2:3], mx[:, 2:3]), reads=['mx2'], writes=['mx2'])
                    nt_ = len(ptiles)
                    ptp = PS[1 + (u % 2)].bitcast(BF16)[:, 0:nt_ * 128]
                    ptt = 'pst%d' % (u % 2)
                    for k in range(nt_):
                        S.op('pe', lambda e, k=k, ptp=ptp, ptiles=ptiles: e.transpose(ptp[:, k * 128:(k + 1) * 128], ptiles[k], ident[:]), reads=[ptag, 'c'], writes=[ptt], inc=(k == nt_ - 1))
                    pts = PTs[u]
                    S.op('dve', lambda e, pts=pts, ptp=ptp, nt_=nt_: e.tensor_copy(pts[:, 0:nt_ * 128], ptp), reads=[ptt], writes=['PTs%d' % u])
                    for k in range(nt_):
                        S.op('pe', lambda e, k=k, pts=pts, vtl=vtl, pr=pr, nt_=nt_: e.matmul(PS[3][:, (u % 2) * 128:(u % 2) * 128 + 128], pts[:, k * 128:(k + 1) * 128], V[:, vtl[k], pr * 128:(pr + 1) * 128], start=(k == 0), stop=(k == nt_ - 1)), reads=['PTs%d' % u, 'V'], writes=['psO%d' % (u % 2)], inc=(k == nt_ - 1))
                    on = onb[u]
                    S.op('dve', lambda e, on=on: e.tensor_scalar(on, PS[3][:, (u % 2) * 128:(u % 2) * 128 + 128], mx[:, 2:3], None, ALU.mult), reads=['psO%d' % (u % 2), 'mx2'], writes=['on%d' % u])
                    for hh in range(2):
                        rs_ = slice(hh * 64, hh * 64 + 64)
                        hcol = 512 + (pr * 2 + hh) * 64
                        S.dma('pool', MIX[b, qpos:qpos + 64, hcol:hcol + 64], on[rs_, rs_], reads=['on%d' % u])
        S.barrier()

    def epilogue(b, j, psq, ptags, bufs, q, last):
        hres, tmp, sq, st = bufs
        if hres[0] is hres[1]:
            q = 0
        mj = 2 if j < 2 else b
        h = hres[q]
        S.dma('sp', h, Hs[b, j * 128:(j + 1) * 128, :], writes=['hr%d' % q])
        for hf in range(2):
            S.op('act', lambda e, hf=hf: e.activation(sq[:, hf * 512:(hf + 1) * 512], psq[:, hf * 512:(hf + 1) * 512], AF.Square, accum_out=st[:, hf:hf + 1]), reads=ptags, writes=['tmp', 'st%d' % hf])
        S.op('dve', lambda e: e.tensor_tensor(st[:, 2:3], st[:, 0:1], st[:, 1:2], ALU.add), reads=['st0', 'st1'], writes=['st2'])
        rstd_from_ss(st[:, 2:3], st[:, 3:4], D, ['st2'], ['st3'])
        for hf in range(2):
            cs = slice(hf * 512, (hf + 1) * 512)
            S.op('dve', lambda e, cs=cs, mj=mj: e.scalar_tensor_tensor(tmp[:, cs], psq[:, cs], st[:, 3:4], MSg[:, mj, cs], ALU.mult, ALU.mult), reads=ptags + ['st3'], writes=['tmp'])
        S.op('pool', lambda e, h=h: e.tensor_tensor(h, h, tmp, ALU.add), reads=['tmp', 'hr%d' % q], writes=['hr%d' % q])
        if last and j >= 2:
            S.dma('pool', y[b, (j - 2) * 128:(j - 1) * 128, :], h, reads=['hr%d' % q])
        else:
            S.dma('pool', Hs[b, j * 128:(j + 1) * 128, :], h, reads=['hr%d' % q])

    def p_outproj(l):
        A.reset()
        wo = A.bf16(8 * D).rearrange("p (k c) -> p k c", k=8)
        stg = [A.f32(4096) for _ in range(2)]
        load_cast(wo, w_out[l].rearrange("(k p) c -> p k c", p=128), 8, D, stg, 'wo')
        S.barrier()
        A.off -= 2 * 4096
        mx_ = [A.bf16(D) for _ in range(2)]
        mT = [A.bf16(D) for _ in range(2)]
        bufs = ([A.f32(D) for _ in range(2)], A.f32(D), A.f32(D), A.f32(4))
        ci = 0
        for b in range(NB):
            for j in range(34):
                q = ci % 2
                ci += 1
                S.dma('sp', mx_[q], MIX[b, j * 128:(j + 1) * 128, :], writes=['mx%d' % q])
                pv = PS[q].bitcast(BF16)[:, 0:1024]
                for k in range(8):
                    S.op('pe', lambda e, k=k, pv=pv, q=q: e.transpose(pv[:, k * 128:(k + 1) * 128], mx_[q][:, k * 128:(k + 1) * 128], ident[:]), reads=['mx%d' % q, 'c'], writes=['pT%d' % q], inc=(k == 7))
                S.op('act', lambda e, pv=pv, q=q: e.activation(mT[q], pv, AF.Copy), reads=['pT%d' % q], writes=['mT%d' % q])
                pa = PS[2 + q]
                for hf in range(2):
                    for k in range(8):
                        S.op('pe', lambda e, k=k, hf=hf, pa=pa, q=q: e.matmul(pa[:, hf * 512:(hf + 1) * 512], mT[q][:, k * 128:(k + 1) * 128], wo[:, k, hf * 512:(hf + 1) * 512], start=(k == 0), stop=(k == 7)), reads=['mT%d' % q, 'wo'], writes=['pa%d' % q], inc=(k == 7 and hf == 1))
                epilogue(b, j, pa, ['pa%d' % q], bufs, q, False)
        S.barrier()

    def p_ffn(l, last):
        A.reset()
        wu = A.bf16(8 * 2 * DFF).rearrange("p (k c) -> p k c", k=8)
        wd = A.bf16(22 * D).rearrange("p (k c) -> p k c", k=22)
        stg = [A.f32(2816) for _ in range(2)]
        load_cast(wu, w_up[l].rearrange("(k p) c -> p k c", p=128), 8, 2 * DFF, stg, 'wu', smax=2816)
        load_cast(wd, w_down[l].rearrange("(k p) c -> p k c", p=128), 22, D, stg, 'wd', smax=2816)
        S.barrier()
        A.off -= 2 * 2816
        if dbg:
            S.dma('pool', DBGW, wd.rearrange("p k c -> p (k c)"), reads=['wd'])
            for k in range(8):
                S.dma('pool', DBGU[:, k * 2 * DFF:(k + 1) * 2 * DFF], wu[:, k, :], reads=['wu'])
        vt1 = A.bf16(8 * 514)
        vts = [vt1, vt1]
        gT = A.bf16(22 * 512).rearrange("p (k c) -> p k c", k=22)
        ctmp = A.f32(512)
        sg = A.f32(512)
        hr1 = A.f32(D)
        tmp1 = A.f32(D)
        bufs = ([hr1, hr1], tmp1, tmp1, A.f32(4))
        si = 0
        ei = 0
        for b in range(NB):
            sts = [(0, 256)] + [(256 + 512 * m, 512) for m in range(8)]
            if last:
                sts = sts[1:]
            for (p0, n) in sts:
                vt_ = vts[si % 2].rearrange("p (k c) -> p k c", k=8)
                vtag = 'vt0'
                si += 1
                c0 = ucol(p0)
                S.dma('sp', vt_[:, :, 0:n + 2], UT[b].rearrange("(k p) c -> p k c", p=128)[:, :, c0 - 1:c0 + n + 1], writes=[vtag])
                npc = n // 256
                for f in range(22):
                    pg = PS[f % 2]
                    pu = PS[2][:, (f % 2) * 512:(f % 2) * 512 + 512]
                    gtag, utag_ = 'pg%d' % (f % 2), 'pu%d' % (f % 2)
                    for pc in range(npc):
                        for k in range(8):
                            S.op('pe', lambda e, pg=pg, pc=pc, k=k, f=f, vt_=vt_: e.matmul(pg[:, pc * 512:pc * 512 + 258], wu[:, k, f * 128:(f + 1) * 128], vt_[:, k, pc * 256:pc * 256 + 258], start=(k == 0), stop=(k == 7)), reads=['wu', vtag], writes=[gtag], inc=(k == 7 and pc == npc - 1))
                    for k in range(8):
                        S.op('pe', lambda e, pu=pu, k=k, f=f, vt_=vt_, n=n: e.matmul(pu[:, 0:n], wu[:, k, DFF + f * 128:DFF + (f + 1) * 128], vt_[:, k, 1:1 + n], start=(k == 0), stop=(k == 7)), reads=['wu', vtag], writes=[utag_], inc=(k == 7))
                    pv = pg.rearrange("p (a c) -> p a c", c=512)[:, 0:npc, :]
                    cv = ctmp[:, 0:n].rearrange("p (a c) -> p a c", c=256)
                    S.op('act', lambda e, pv=pv, cv=cv, f=f: e.activation(cv, pv[:, :, 0:256], AF.Copy, scale=fcw[:, f, 0:1]), reads=[gtag, 'fcw'], writes=['ctmp'])
                    S.op('dve', lambda e, pv=pv, cv=cv, f=f: e.scalar_tensor_tensor(cv, pv[:, :, 1:257], fcw[:, f, 1:2], cv, ALU.mult, ALU.add), reads=[gtag, 'ctmp'], writes=['ctmp'])
                    S.op('dve', lambda e, pv=pv, cv=cv, f=f: e.scalar_tensor_tensor(cv, pv[:, :, 2:258], fcw[:, f, 2:3], cv, ALU.mult, ALU.add), reads=[gtag, 'ctmp'], writes=['ctmp'])
                    S.op('act', lambda e, f=f, n=n: e.activation(sg[:, 0:n], ctmp[:, 0:n], AF.Silu, bias=fcw[:, f, 3:4]), reads=['ctmp'], writes=['sg'])
                    S.op('dve', lambda e, f=f, n=n, pu=pu: e.tensor_tensor(gT[:, f, 0:n], sg[:, 0:n], pu[:, 0:n], ALU.mult), reads=['sg', utag_], writes=['gT'])
                for s_ in range(n // 128):
                    q = ei % 2
                    ei += 1
                    j = (p0 + s_ * 128) // 128
                    pa = PS[3] if q == 0 else PS[2]
                    ptg = ['pa%d' % q] if q == 0 else ['pu0', 'pu1']
                    for hf in range(2):
                        for k in range(22):
                            S.op('pe', lambda e, k=k, hf=hf, pa=pa, s_=s_: e.matmul(pa[:, hf * 512:(hf + 1) * 512], gT[:, k, s_ * 128:(s_ + 1) * 128], wd[:, k, hf * 512:(hf + 1) * 512], start=(k == 0), stop=(k == 21)), reads=['gT', 'wd'], writes=ptg, inc=(k == 21 and hf == 1))
                    epilogue(b, j, pa, ptg, bufs, q, last)
        S.barrier()

    p_init()
    for l in range(NL):
        last = (l == 3)
        p_small(l)
        p_ada(l, 0)
        if stop_after == 'ada':
            break
        p_normT(l)
        if stop_after == 'normT':
            break
        p_inproj(l)
        if stop_after == 'inproj':
            break
        p_ssd(l, 0)
        p_ssd(l, 1)
        if stop_after == 'ssd':
            break
        p_hg(l, 0)
        p_hg(l, 1)
        if stop_after == 'hg':
            break
        p_na(l)
        if stop_after == 'na':
            break
        p_outproj(l)
        if stop_after == 'outproj':
            break
        p_ada(l, 1)
        p_normT(l)
        if stop_after == 'normT2':
            break
        p_ffn(l, last)
    S.finish()
    return nc, S


def host_prep(inputs):
    f = lambda a: np.ascontiguousarray(np.asarray(a, dtype=np.float32))
    rpb = f(inputs['na_rpb'])
    j = np.arange(64)[:, None]
    c = np.arange(64)[None, :]
    idx = np.clip(c - j + 15, 0, 30)
    g = rpb[:, :, :, idx]
    g = np.transpose(g, (0, 1, 3, 2, 4)).reshape(4, 2, 128, 15 * 64)
    start = np.clip(np.arange(64) - 8, 0, 48)[:, None]
    inwin = (c >= start) & (c < start + 16)
    mask = np.where(inwin, 0.0, -30000.0).astype(np.float32)
    mask = np.concatenate([mask, mask], axis=0)
    shared = {
        'c_ctx': f(inputs['c_ctx']).reshape(1, D),
        'ada_w': f(inputs['ada_w']),
        'ada_b': f(inputs['ada_b']).reshape(4, 1, 6 * D),
        'norms': np.ascontiguousarray(np.stack([f(inputs['norm_mix_pre']), f(inputs['norm_mix_post']), f(inputs['norm_ffn_pre']), f(inputs['norm_ffn_post'])], axis=1)),
        'w_in': f(inputs['w_in']),
        'ssd_cw': np.ascontiguousarray(np.concatenate([f(inputs['ssd_conv_w']), f(inputs['ssd_conv_b'])[:, None, :]], axis=1)),
        'ssd_dtb': f(inputs['ssd_dt_bias']).reshape(4, 1, 16),
        'ssd_alog': f(inputs['ssd_a_log']).reshape(4, 1, 16),
        'ssd_d': f(inputs['ssd_d']).reshape(4, 1, 8),
        'ssd_norm': f(inputs['ssd_norm']).reshape(4, 1, 512),
        'rpbT': np.ascontiguousarray(g),
        'namask': mask,
        'hg_lbl': f(inputs['hg_lb_logits']),
        'hg_norm': f(inputs['hg_norm']).reshape(4, 1, 64),
        'w_out': f(inputs['w_out']),
        'w_up': f(inputs['ffn_w_up']),
        'ffn_cw': np.ascontiguousarray(np.concatenate([f(inputs['ffn_conv_w']), f(inputs['ffn_conv_b'])[:, None, :]], axis=1)),
        'w_down': f(inputs['ffn_w_down']),
    }
    return shared


def kernel(**inputs):
    shared = host_prep(inputs)
    x = np.asarray(inputs['x'], dtype=np.float32)
    c = np.asarray(inputs['c'], dtype=np.float32)
    ctx = np.asarray(inputs['ctx'], dtype=np.float32)
    nc, _ = build()
    in_maps = []
    for i in range(8):
        m = dict(shared)
        m['x'] = np.ascontiguousarray(x[2 * i:2 * i + 2])
        m['c'] = np.ascontiguousarray(c[2 * i:2 * i + 2])
        m['ctx'] = np.ascontiguousarray(ctx[2 * i:2 * i + 2])
        in_maps.append(m)
    res = run_bass_kernel_spmd(nc, in_maps, core_ids=list(range(8)))
    return np.concatenate([np.asarray(r['y'], dtype=np.float32) for r in res.results], axis=0)
```
